# Optimizing a Trainium2 kernel written in Bass

```python
import jax
import jax.numpy as jnp
from jax import lax
import numpy as np

D_MODEL = 1024
BATCH = 8
SEQ = 4096
DEPTH = 2

HEAD_DIM = 64
ATT_WIDTH = 3 * D_MODEL // 8
RET_WIDTH = 3 * D_MODEL // 8
HGRN_WIDTH = D_MODEL - ATT_WIDTH - RET_WIDTH
ATT_HEADS = ATT_WIDTH // HEAD_DIM
RET_HEADS = RET_WIDTH // HEAD_DIM
HGRN_HEADS = HGRN_WIDTH // HEAD_DIM
MIX_WIDTH = ATT_WIDTH + RET_WIDTH + HGRN_WIDTH
IN_SPLITS = (ATT_WIDTH,) * 3 + (RET_WIDTH,) * 4 + (HGRN_WIDTH,) * 5
IN_WIDTH = sum(IN_SPLITS)
DILATED_CONFIGS = ((128, 1), (512, 4), (2048, 16))
RET_CHUNK = 128
HGRN_CHUNK = 16
FFN_HIDDEN = ((-(-8 * D_MODEL // 3)) + 255) // 256 * 256
N_MOD = 6
NORM_EPS = 1e-6
NEG_INF = -1e30
LB_FLOOR = 1e-30

kernel_name = 'hybrid_dilated_retention_hgrn2_encoder'


def rms_norm(x, gain):
    xf = x.astype(jnp.float32)
    y = xf * lax.rsqrt(jnp.mean(xf * xf, axis=-1, keepdims=True) + NORM_EPS)
    return (y * gain.astype(jnp.float32)).astype(x.dtype)


def dilated_branch(q, k, v, slopes, window, dilation):
    B, S, H, dh = q.shape
    r = dilation
    n_side = window // 2 // r
    L = S // r
    nb = -(-L // n_side)
    Lp = nb * n_side

    def to_sub(t):
        t = t.reshape(B, L, r, H, dh).transpose(0, 2, 3, 1, 4)
        return jnp.pad(t, ((0, 0), (0, 0), (0, 0), (0, Lp - L), (0, 0)))

    def key_blocks(t):
        t = jnp.pad(to_sub(t), ((0, 0), (0, 0), (0, 0), (n_side, n_side), (0, 0)))
        t = t.reshape(B, r, H, nb + 2, n_side, dh)
        return jnp.concatenate([t[:, :, :, :-2], t[:, :, :, 1:-1], t[:, :, :, 2:]], axis=4)

    qs = to_sub(q).reshape(B, r, H, nb, n_side, dh) * HEAD_DIM ** -0.5
    ks, vs = key_blocks(k), key_blocks(v)
    qi = jnp.arange(Lp).reshape(nb, n_side, 1)
    kj = jnp.arange(nb)[:, None, None] * n_side + jnp.arange(3 * n_side)[None, None, :] - n_side
    steps = jnp.abs(qi - kj)
    valid = (steps <= n_side) & (kj >= 0) & (kj < L)
    bias = -slopes[:, None, None, None] * (dilation * steps).astype(jnp.float32)
    s = jnp.einsum('brhnqd,brhnkd->brhnqk', qs, ks, preferred_element_type=jnp.float32) + bias
    s = jnp.where(valid, s, NEG_INF)
    m = jnp.max(s, axis=-1, keepdims=True)
    p = jnp.exp(s - m)
    l = jnp.sum(p, axis=-1, keepdims=True)
    o = jnp.einsum('brhnqk,brhnkd->brhnqd', p, vs.astype(jnp.float32)) / l
    lse = (m + jnp.log(l))[..., 0]
    o = o.reshape(B, r, H, Lp, dh)[:, :, :, :L].transpose(0, 3, 1, 2, 4).reshape(B, S, H, dh)
    lse = lse.reshape(B, r, H, Lp)[..., :L].transpose(0, 3, 1, 2).reshape(B, S, H)
    return o, lse


def retention_bidir(q, k, v):
    B, S, H, d = q.shape
    C = RET_CHUNK
    N = S // C

    def chunks(t):
        return t.astype(jnp.float32).reshape(B, N, C, H, d).transpose(0, 3, 1, 2, 4)

    q, k, v = chunks(q), chunks(k) * d ** -0.5, chunks(v)
    log_g = jnp.log1p(-jnp.exp2(-5.0 - jnp.arange(H, dtype=jnp.float32)))[:, None, None, None]
    pos = jnp.arange(C, dtype=jnp.float32)[:, None]
    intra = jnp.exp(log_g * jnp.abs(pos - pos.T))
    scores = jnp.einsum('bhnid,bhnjd->bhnij', q, k) * intra
    out = jnp.einsum('bhnij,bhnjd->bhnid', scores, v)
    u_fwd = jnp.einsum('bhncd,bhnce->nbhde', k * jnp.exp(log_g * (C - 1 - pos)), v)
    u_bwd = jnp.einsum('bhncd,bhnce->nbhde', k * jnp.exp(log_g * pos), v)
    chunk_decay = jnp.exp(log_g[:, :, 0] * C)

    def step(state, u):
        return chunk_decay * state + u, state

    zero = jnp.zeros((B, H, d, d), jnp.float32)
    _, r_fwd = lax.scan(step, zero, u_fwd)
    _, r_bwd = lax.scan(step, zero, u_bwd, reverse=True)
    out = (out
           + jnp.einsum('bhncd,nbhde->bhnce', q * jnp.exp(log_g * (pos + 1.0)), r_fwd)
           + jnp.einsum('bhncd,nbhde->bhnce', q * jnp.exp(log_g * (C - pos)), r_bwd))
    return out.transpose(0, 2, 3, 1, 4).reshape(B, S, H, d)


def gated_recurrence_chunked(q, log_f, k, v):
    B, S, H, dk = q.shape
    dv = v.shape[-1]
    C = HGRN_CHUNK
    N = S // C

    def chunks(t):
        return t.reshape(B, N, C, H, t.shape[-1]).transpose(0, 3, 1, 2, 4)

    q, log_f, k, v = chunks(q), chunks(log_f), chunks(k), chunks(v)
    b = jnp.cumsum(log_f, axis=3)
    tri = jnp.tril(jnp.ones((C, C), dtype=bool))[:, :, None]
    rel = jnp.where(tri, jnp.exp(jnp.minimum(b[:, :, :, :, None, :] - b[:, :, :, None, :, :], 0.0)), 0.0)
    attn = jnp.einsum('bhntd,bhnsd,bhntsd->bhnts', q, k, rel)
    out = jnp.einsum('bhnts,bhnse->bhnte', attn, v)
    b_last = b[:, :, :, -1:, :]
    u = jnp.einsum('bhnsd,bhnse->nbhde', k * jnp.exp(b_last - b), v)
    a = jnp.exp(b_last[:, :, :, 0, :]).transpose(2, 0, 1, 3)[..., None]

    def step(state, xs):
        a_n, u_n = xs
        return a_n * state + u_n, state

    _, s_prev = lax.scan(step, jnp.zeros((B, H, dk, dv), jnp.float32), (a, u))
    out = out + jnp.einsum('bhntd,nbhde->bhnte', q * jnp.exp(b), s_prev)
    return out.transpose(0, 2, 3, 1, 4).reshape(B, S, H, dv)


def hgrn2_bidir(hq, hz_fwd, hz_bwd, hi, lb):
    B, S, _ = hq.shape
    shp = (B, S, HGRN_HEADS, HEAD_DIM)
    log_lb = jnp.log(jnp.maximum(lb, LB_FLOOR))
    log_one_minus_lb = jnp.log1p(-lb)

    def gates(z):
        z = z.astype(jnp.float32)
        log_f = jnp.logaddexp(log_lb, log_one_minus_lb + jax.nn.log_sigmoid(z))
        k = (1.0 - lb) * jax.nn.sigmoid(-z)
        return log_f.reshape(shp), k.reshape(shp)

    q = jax.nn.silu(hq.astype(jnp.float32)).reshape(shp)
    v = hi.astype(jnp.float32).reshape(shp)
    lf_f, k_f = gates(hz_fwd)
    lf_b, k_b = gates(hz_bwd)
    o_f = gated_recurrence_chunked(q, lf_f, k_f, v)
    o_b = jnp.flip(gated_recurrence_chunked(jnp.flip(q, 1), jnp.flip(lf_b, 1), jnp.flip(k_b, 1), jnp.flip(v, 1)), 1)
    return (o_f + o_b).reshape(B, S, HGRN_WIDTH)


def hybrid_mixer(h, w_in, w_out, ret_gn, hgrn_gn, lb):
    B, S, _ = h.shape
    proj = h @ w_in
    cuts = [int(i) for i in np.cumsum(IN_SPLITS)[:-1]]
    aq, ak, av, rq, rk, rv, rg, hq, hzf, hzb, hi, hg = jnp.split(proj, cuts, axis=-1)

    def heads(t, n):
        return t.reshape(B, S, n, HEAD_DIM)

    q, k, v = heads(aq, ATT_HEADS), heads(ak, ATT_HEADS), heads(av, ATT_HEADS)
    slopes = jnp.exp2(-8.0 * jnp.arange(1, ATT_HEADS + 1, dtype=jnp.float32) / ATT_HEADS)
    outs, lses = [], []
    for window, dilation in DILATED_CONFIGS:
        o, l = dilated_branch(q, k, v, slopes, window, dilation)
        outs.append(o)
        lses.append(l)
    wts = jax.nn.softmax(jnp.stack(lses), axis=0)[..., None]
    att = jnp.sum(wts * jnp.stack(outs), axis=0).reshape(B, S, ATT_WIDTH)

    ret = retention_bidir(heads(rq, RET_HEADS), heads(rk, RET_HEADS), heads(rv, RET_HEADS))
    mu = jnp.mean(ret, axis=-1, keepdims=True)
    var = jnp.mean(jnp.square(ret - mu), axis=-1, keepdims=True)
    ret = ((ret - mu) * lax.rsqrt(var + NORM_EPS)).reshape(B, S, RET_WIDTH)
    ret = ret * ret_gn.astype(jnp.float32) * jax.nn.silu(rg.astype(jnp.float32))

    hgo = rms_norm(hgrn2_bidir(hq, hzf, hzb, hi, lb), hgrn_gn) * jax.nn.silu(hg.astype(jnp.float32))

    mixed = jnp.concatenate([att, ret, hgo], axis=-1).astype(h.dtype)
    return mixed @ w_out


def swiglu(h, w_gate_up, w_down):
    g, u = jnp.split(h @ w_gate_up, 2, axis=-1)
    return (jax.nn.silu(g) * u) @ w_down


def setup_inputs(seed: int = 0) -> dict:
    key = jax.random.key(seed)
    ks = jax.random.split(key, 16)

    def nrm(k, shape, scale):
        return jax.random.normal(k, shape, jnp.float32) * scale

    D = D_MODEL
    return {
        'x': nrm(ks[0], (BATCH, SEQ, D), 1.0),
        'c': nrm(ks[1], (BATCH, D), 1.0),
        'w_ada': nrm(ks[2], (DEPTH, D, N_MOD * D), 0.5 * D ** -0.5),
        'b_ada': nrm(ks[3], (DEPTH, N_MOD * D), 0.02),
        'g_mix': 1.0 + nrm(ks[4], (DEPTH, D), 0.02),
        'w_in': nrm(ks[5], (DEPTH, D, IN_WIDTH), D ** -0.5),
        'ret_gn': 1.0 + nrm(ks[6], (DEPTH, RET_WIDTH), 0.02),
        'hgrn_gn': 1.0 + nrm(ks[7], (DEPTH, HGRN_WIDTH), 0.02),
        'hgrn_lb_logits': nrm(ks[8], (DEPTH, HGRN_WIDTH), 0.5),
        'w_out': nrm(ks[9], (DEPTH, MIX_WIDTH, D), MIX_WIDTH ** -0.5),
        'g_ffn': 1.0 + nrm(ks[10], (DEPTH, D), 0.02),
        'w_gate_up': nrm(ks[11], (DEPTH, D, 2 * FFN_HIDDEN), D ** -0.5),
        'w_down': nrm(ks[12], (DEPTH, FFN_HIDDEN, D), FFN_HIDDEN ** -0.5),
        'g_final': 1.0 + nrm(ks[13], (D,), 0.02),
    }


def reference(x, c, w_ada, b_ada, g_mix, w_in, ret_gn, hgrn_gn, hgrn_lb_logits, w_out,
              g_ffn, w_gate_up, w_down, g_final):
    B, S, D = x.shape
    lb_p = jax.nn.softmax(hgrn_lb_logits.astype(jnp.float32), axis=0)
    lb_all = jnp.clip(jnp.cumsum(lb_p, axis=0) - lb_p[0:1], 0.0, 1.0 - 1e-6)
    cond = jax.nn.silu(c)
    for layer in range(DEPTH):
        mod = (cond @ w_ada[layer] + b_ada[layer]).reshape(B, N_MOD, 1, D)
        shift1, scale1, gate1 = mod[:, 0], mod[:, 1], mod[:, 2]
        shift2, scale2, gate2 = mod[:, 3], mod[:, 4], mod[:, 5]
        h = rms_norm(x, g_mix[layer]) * (1.0 + scale1) + shift1
        y = hybrid_mixer(h, w_in[layer], w_out[layer], ret_gn[layer], hgrn_gn[layer], lb_all[layer])
        x = x + (gate1 * y).astype(x.dtype)
        h = rms_norm(x, g_ffn[layer]) * (1.0 + scale2) + shift2
        x = x + (gate2 * swiglu(h, w_gate_up[layer], w_down[layer])).astype(x.dtype)
    return rms_norm(x, g_final)
```

```python
from contextlib import ExitStack
import numpy as np
import concourse.bass as bass
import concourse.mybir as mybir
from concourse.bass_utils import run_bass_kernel_spmd

F32 = mybir.dt.float32
BF16 = mybir.dt.bfloat16
AF = mybir.ActivationFunctionType
ALU = mybir.AluOpType

PE, ACT, DVE, POOL, SP = "tensor", "scalar", "vector", "gpsimd", "sync"
ENGS = (PE, ACT, DVE, POOL, SP)
NDMASEM = 8

D = 1024
S = 4096
NT = 32
DEPTH = 2
INW = 3968
FH = 2816
EPS = 1e-6
C_AQ, C_AK, C_AV, C_RQ, C_RK, C_RV, C_RG, C_HQ, C_HZF, C_HZB, C_HI, C_HG = (
    0, 384, 768, 1152, 1536, 1920, 2304, 2688, 2944, 3200, 3456, 3712)
VPAD = 1024
CH = 32
NCH = S // CH
CPT = 128 // CH


class Prog:
    def __init__(self, nc, es):
        self.nc = nc
        self.es = es
        self.ops = {e: [] for e in ENGS}
        self.cnt = {e: 0 for e in ENGS}
        self.sem = {e: es.enter_context(nc.semaphore("s_" + e)) for e in (PE, ACT, DVE, POOL)}
        self.dsem, self.dcnt, self.dnext = {}, {}, {}
        for q in (SP, POOL, ACT):
            self.dsem[q] = [es.enter_context(nc.semaphore(f"d_{q}_{i}")) for i in range(NDMASEM)]
            self.dcnt[q] = [0] * NDMASEM
            self.dnext[q] = 0
        self.waited = {e: {} for e in ENGS}
        self.last_w = {}
        self.readers = {}
        self.pending_noinc = {e: False for e in ENGS}

    def sb(self, es, name, shape, dt):
        self.uid = getattr(self, "uid", 0) + 1
        return es.enter_context(self.nc.sbuf_tensor(f"{name}_{self.uid}", list(shape), dt))

    def ps(self, es, name, shape, dt):
        self.uid = getattr(self, "uid", 0) + 1
        return es.enter_context(self.nc.psum_tensor(f"{name}_{self.uid}", list(shape), dt))

    def _need(self, eng, sk, val):
        if self.waited[eng].get(sk, 0) >= val:
            return
        self.waited[eng][sk] = val
        self.ops[eng].append(("wait", sk, val))

    def _deps(self, eng, reads, writes):
        toks = []
        for k in reads:
            lw = self.last_w.get(k)
            if lw is not None:
                toks.append(lw)
        for k in writes:
            lw = self.last_w.get(k)
            if lw is not None:
                toks.append(lw)
            toks.extend(self.readers.get(k, ()))
        for t in toks:
            if t[0] == eng and eng == PE:
                continue
            self._need(eng, t[1], t[2])

    def _commit(self, tok, reads, writes):
        for k in writes:
            self.last_w[k] = tok
            self.readers[k] = []
        for k in reads:
            self.readers.setdefault(k, []).append(tok)

    def op(self, eng, fn, reads=(), writes=(), inc=True):
        self._deps(eng, reads, writes)
        val = self.cnt[eng] + 1
        if inc:
            self.cnt[eng] = val
        self.pending_noinc[eng] = not inc
        self.ops[eng].append(("op", fn, inc))
        self._commit((eng, ("c", eng), val), reads, writes)

    def dma(self, q, fn, reads=(), writes=()):
        self._deps(q, reads, writes)
        i = self.dnext[q]
        self.dnext[q] = (i + 1) % NDMASEM
        if self.dcnt[q][i] > 0:
            self._need(q, ("d", q, i), self.dcnt[q][i])
        self.dcnt[q][i] += 16
        self.ops[q].append(("dma", fn, i))
        self._commit(("dma_" + q, ("d", q, i), self.dcnt[q][i]), reads, writes)

    def barrier(self):
        for e in ENGS:
            assert not self.pending_noinc[e], e
        for eng in ENGS:
            for e2 in (PE, ACT, DVE, POOL):
                if e2 != eng and self.cnt[e2] > 0:
                    self._need(eng, ("c", e2), self.cnt[e2])
            for q in self.dsem:
                for i in range(NDMASEM):
                    if self.dcnt[q][i] > 0:
                        self._need(eng, ("d", q, i), self.dcnt[q][i])
        self.last_w = {}
        self.readers = {}
        self.emit()

    def _semh(self, sk):
        return self.sem[sk[1]] if sk[0] == "c" else self.dsem[sk[1]][sk[2]]

    def emit(self):
        with self.nc.Block() as block:
            for eng in ENGS:
                def body(e, ops=self.ops[eng], eng=eng):
                    for o in ops:
                        if o[0] == "wait":
                            e.wait_ge(self._semh(o[1]), o[2])
                        elif o[0] == "op":
                            ins = o[1](e)
                            if o[2]:
                                ins.then_inc(self.sem[eng], 1)
                        else:
                            o[1](e).then_inc(self.dsem[eng][o[2]], 16)
                getattr(block, eng)(body)
        self.ops = {e: [] for e in ENGS}

    def act(self, out, in_, func, reads, writes, eng=ACT, **kw):
        self.op(eng, lambda e: e.activation(out=out, in_=in_, func=func, **kw), reads, writes)

    def tt(self, eng, out, in0, in1, op, reads, writes):
        self.op(eng, lambda e: e.tensor_tensor(out=out, in0=in0, in1=in1, op=op), reads, writes)

    def ts(self, eng, out, in0, s1, s2, op0, op1, reads, writes):
        if s2 is None:
            self.op(eng, lambda e: e.tensor_scalar(out=out, in0=in0, scalar1=s1, scalar2=None, op0=op0), reads, writes)
        else:
            self.op(eng, lambda e: e.tensor_scalar(out=out, in0=in0, scalar1=s1, scalar2=s2, op0=op0, op1=op1), reads, writes)

    def stt(self, eng, out, in0, scalar, in1, op0, op1, reads, writes):
        self.op(eng, lambda e: e.scalar_tensor_tensor(out=out, in0=in0, scalar=scalar, in1=in1, op0=op0, op1=op1), reads, writes)

    def cp(self, eng, out, in_, reads, writes):
        if eng == ACT:
            self.op(eng, lambda e: e.activation(out=out, in_=in_, func=AF.Copy), reads, writes)
        else:
            self.op(eng, lambda e: e.tensor_copy(out=out, in_=in_), reads, writes)

    def mm(self, out, lhsT, rhs, start, stop, reads, writes, inc=True):
        self.op(PE, lambda e: e.matmul(out, lhsT=lhsT, rhs=rhs, start=start, stop=stop), reads, writes, inc=inc)

    def tr(self, out, in_, ident, reads, writes, inc=True):
        self.op(PE, lambda e: e.transpose(out=out, in_=in_, identity=ident), reads, writes, inc=inc)

    def memset(self, eng, ap, val, writes):
        self.op(eng, lambda e: e.memset(ap, val), (), writes)

    def ld(self, q, out, in_, reads, writes, slow=False):
        if slow:
            self.dma(q, lambda e: e.dma_start(out=out, in_=in_, allow_slow_non_contiguous=True), reads, writes)
        else:
            self.dma(q, lambda e: e.dma_start(out=out, in_=in_), reads, writes)


def sst(start, n, step=1):
    return slice(start, start + step * (n - 1) + 1, step)


class RR:
    def __init__(self, items):
        self.items = list(items)
        self.i = 0

    def __call__(self):
        v = self.items[self.i % len(self.items)]
        self.i += 1
        return v


def make_tables():
    t = {}
    t["ident"] = np.eye(128, dtype=np.float32)
    slopes = 2.0 ** (-8.0 * np.arange(1, 7) / 6.0)
    ab = np.zeros((18, 128, 256), np.float32)
    ip = np.arange(128)[:, None]
    iq = np.arange(128)[None, :]
    for h in range(6):
        for ri, r in enumerate((1, 4, 16)):
            for jj in range(2):
                d = (ip - 64 + 128 * jj) - iq
                b = -slopes[h] * r * np.abs(d).astype(np.float64)
                b = np.where(np.abs(d) <= 64, b, -30000.0)
                ab[h * 3 + ri, :, jj * 128:(jj + 1) * 128] = b
    t["attb"] = ab.transpose(1, 0, 2).copy()
    gam = 1.0 - 2.0 ** (-5.0 - np.arange(6))
    lg = np.log1p(-(2.0 ** (-5.0 - np.arange(6))))
    pos = np.arange(128, dtype=np.float64)
    retd = np.zeros((128, 6, 128), np.float32)
    retg = np.zeros((128, 6, 128), np.float32)
    retk = np.zeros((128, 6, 2), np.float32)
    for h in range(6):
        retd[:, h, :] = np.exp(lg[h] * np.abs(pos[:, None] - pos[None, :]))
        retg[0:64, h, :] = np.exp(lg[h] * (pos + 1.0))[None, :]
        retg[64:128, h, :] = np.exp(lg[h] * (128.0 - pos))[None, :]
        retk[:, h, 0] = np.exp(lg[h] * (127.0 - pos))
        retk[:, h, 1] = np.exp(lg[h] * pos)
    t["retd"], t["retg"], t["retk"] = retd, retg, retk
    t["ret_cd"] = [float(np.exp(lg[h] * 128.0)) for h in range(6)]
    s_ = np.arange(128)[:, None]
    t_ = np.arange(128)[None, :]
    same = (s_ // CH) == (t_ // CH)
    hm = np.zeros((128, 2, 128), np.float32)
    hm[:, 0, :] = (same & (s_ <= t_)).astype(np.float32)
    hm[:, 1, :] = (same & (s_ >= t_)).astype(np.float32)
    t["hmask"] = hm
    cm = np.zeros((128, CPT), np.float32)
    cm[np.arange(128), np.arange(128) // CH] = 1.0
    t["cmask"] = cm
    o64 = np.zeros((128, 128), np.float32)
    o64[0:64, 0:64] = 1.0 / 64
    o64[64:128, 64:128] = 1.0 / 64
    t["ones64"] = o64
    t["ones256"] = np.full((128, 128), 1.0 / 256, np.float32)
    return t


TABLE_SHAPES = {"ident": [128, 128], "attb": [128, 18, 256], "retd": [128, 6, 128], "retg": [128, 6, 128],
                "retk": [128, 6, 2], "hmask": [128, 2, 128], "cmask": [128, CPT], "ones64": [128, 128],
                "ones256": [128, 128]}


def build(stop_after=None, dbg=()):
    nc = bass.Bass("TRN2", target_bir_lowering=False)
    tabs = make_tables()

    def din(name, shape):
        return nc.dram_tensor(name, list(shape), F32, kind="ExternalInput").ap()

    def dscr(name, shape, dt):
        kind = "ExternalOutput" if name in dbg else "Internal"
        return nc.dram_tensor(name, list(shape), dt, kind=kind).ap()

    x_in = din("x", [S, D])
    c_in = din("c", [1, D])
    w_ada = din("w_ada", [DEPTH, D, 6 * D])
    b_ada = din("b_ada", [DEPTH, 6 * D])
    g_mix = din("g_mix", [DEPTH, D])
    w_in = din("w_in", [DEPTH, D, INW])
    ret_gn = din("ret_gn", [DEPTH, 384])
    hgrn_gn = din("hgrn_gn", [DEPTH, 256])
    lb_log = din("hgrn_lb_logits", [DEPTH, 256])
    w_out = din("w_out", [DEPTH, D, D])
    g_ffn = din("g_ffn", [DEPTH, D])
    w_gu = din("w_gate_up", [DEPTH, D, 2 * FH])
    w_dn = din("w_down", [DEPTH, FH, D])
    g_fin = din("g_final", [1, D])
    T = {k: din("t_" + k, shp) for k, shp in TABLE_SHAPES.items()}
    out = nc.dram_tensor("out", [S, D], F32, kind="ExternalOutput").ap()

    modscr = dscr("modscr", [DEPTH, 6 * D], F32)
    xa = dscr("xa", [S, D], F32)
    xb = dscr("xb", [S, D], F32)
    aqT = dscr("aqT", [384, S], BF16)
    akT = dscr("akT", [384, S], BF16)
    vaug = dscr("vaug", [S + 2 * VPAD, 6, 128], BF16)
    rqT = dscr("rqT", [384, S], BF16)
    rkT = dscr("rkT", [384, S], BF16)
    rk = dscr("rk", [S, 384], BF16)
    rv = dscr("rv", [S, 384], BF16)
    rgT = dscr("rgT", [384, S], F32)
    hqT = dscr("hqT", [256, S], F32)
    zfT = dscr("zfT", [256, S], F32)
    zbT = dscr("zbT", [256, S], F32)
    hgT = dscr("hgT", [256, S], F32)
    hv = dscr("hv", [S, 256], BF16)
    hvblk = dscr("hvblk", [NT, 128, 4, CPT, 64], BF16)
    ohT = dscr("ohT", [256, S], F32)
    mixT = dscr("mixT", [D, S], BF16)
    wdn_bf = dscr("wdn_bf", [FH, D], BF16)

    with ExitStack() as es:
        P = Prog(nc, es)
        identf = P.sb(es, "identf", [128, 128], F32)
        identb = P.sb(es, "identb", [128, 128], BF16)
        ss3 = P.sb(es, "ss3", [128, NT], F32)
        P.ld(SP, identf[:], T["ident"][:, :], (), ["identf"])
        P.cp(DVE, identb[:], identf[:], ["identf"], ["identb"])
        P.barrier()

        def done(tag):
            return stop_after == tag

        def col_load(ph, dst, src2d, n, ps_ap, ps_key, name, dkey):
            tmp = P.sb(ph, name, [n, 128], F32)
            P.ld(SP, tmp[:], src2d, (), [name])
            P.tr(ps_ap[:, 0:n], tmp[:], identf[0:n, 0:n], [name, "identf"], [ps_key])
            P.cp(DVE, dst, ps_ap[:, 0:n], [ps_key], [dkey])

        finished = False

        with ExitStack() as ph:
            ccol = P.sb(ph, "ccol", [128, 8], F32)
            cond = P.sb(ph, "cond", [128, 8], F32)
            ctmp = P.sb(ph, "ctmp", [128, 8], F32)
            brow = P.sb(ph, "brow", [1, 6 * D], F32)
            mrow = P.sb(ph, "mrow", [1, 6 * D], F32)
            wt = [P.sb(ph, f"wada{i}", [128, 8, 512], F32) for i in range(3)]
            pm = [P.ps(ph, f"pm_m{i}", [128, 512], F32) for i in range(2)]
            col_load(ph, ccol[:], c_in.rearrange("o (k p) -> (o k) p", p=128), 8, pm[0], "pm_m0", "c_rows", "ccol")
            P.act(ctmp[:], ccol[:], AF.Exp, ["ccol"], ["ctmp"], scale=-1.0)
            P.ts(DVE, ctmp[:], ctmp[:], 1.0, None, ALU.add, None, ["ctmp"], ["ctmp"])
            P.op(DVE, lambda e: e.reciprocal(out=ctmp[:], in_=ctmp[:]), ["ctmp"], ["ctmp"])
            P.tt(DVE, cond[:], ccol[:], ctmp[:], ALU.mult, ["ccol", "ctmp"], ["cond"])
            xm = [P.sb(ph, f"xm{i}", [128, D], F32) for i in range(3)]
            jm = P.sb(ph, "jm", [128, D], BF16)
            for t in range(NT):
                P.ld(POOL, xm[t % 3][:], x_in[t * 128:(t + 1) * 128, :], (), [f"xm{t % 3}"])
                P.act(jm[:], xm[t % 3][:], AF.Square, [f"xm{t % 3}"], ["jm", "ss3"], accum_out=ss3[:, t:t + 1])
            it = 0
            for L in range(DEPTH):
                P.ld(SP, brow[:], b_ada[L:L + 1, :], ["mrow_done"], ["brow"])
                for n in range(12):
                    w = wt[it % 3]
                    p_ = pm[it % 2]
                    P.ld(SP, w[:],
                         w_ada[L, :, n * 512:(n + 1) * 512].rearrange("(kc kp) n -> kp kc n", kp=128),
                         (), [f"wada{it % 3}"])
                    for k in range(8):
                        P.mm(p_[0:1, :], cond[:, k:k + 1], w[:, k, :], k == 0, k == 7,
                             ["cond", f"wada{it % 3}"], [f"pm_m{it % 2}"], inc=(k == 7))
                    P.tt(DVE, mrow[0:1, n * 512:(n + 1) * 512], p_[0:1, :], brow[0:1, n * 512:(n + 1) * 512],
                         ALU.add, [f"pm_m{it % 2}", "brow"], ["mrow"])
                    it += 1
                P.ld(SP, modscr[L:L + 1, :], mrow[:], ["mrow"], ["modscr", "mrow_done"])
            P.barrier()
        if done("M"):
            finished = True

        x_cur = x_in
        for L in range(DEPTH):
            if finished:
                break
            with ExitStack() as ph:
                hT = P.sb(ph, "hT", [128, 8, S], BF16)
                modc = P.sb(ph, "modc", [128, 48], F32)
                gcol = P.sb(ph, "gcol", [128, 8], F32)
                gs = P.sb(ph, "gs", [128, 8], F32)
                ss = P.sb(ph, "ss", [128, NT], F32)
                rstd = P.sb(ph, "rstd", [128, NT], F32)
                junk = P.sb(ph, "junk", [128, D], BF16)
                xt = [P.sb(ph, f"xt{i}", [128, D], F32) for i in range(3)]
                xn = [P.sb(ph, f"xn{i}", [128, D], BF16) for i in range(3)]
                wtm = P.sb(ph, "wtm", [128, 8, 1408], BF16)
                wfm = [P.sb(ph, f"wfm{i}", [128, 8, 128], BF16) for i in range(3)]
                stg_b = [P.sb(ph, f"stgb{i}", [128, 512], BF16) for i in range(8)]
                stg_f = [P.sb(ph, f"stgf{i}", [128, 512], F32) for i in range(6)]
                stg_v = [P.sb(ph, f"stgv{i}", [128, 6, 128], BF16) for i in range(2)]
                zt = P.sb(ph, "zt", [128, 768], BF16)
                cmk = P.sb(ph, "cmk", [128, CPT], F32)
                vb4 = [P.sb(ph, f"vb4{i}", [128, 4, CPT, 64], BF16) for i in range(2)]
                ptr = [P.ps(ph, f"ptr{i}", [128, 8, 128], BF16) for i in range(2)]
                pp = [P.ps(ph, f"pp{i}", [128, 512], F32) for i in range(6)]

                col_load(ph, modc[:], modscr[L:L + 1, :].rearrange("o (m p) -> (o m) p", p=128), 48, pp[0], "pp0", "mod_rows", "modc")
                col_load(ph, gcol[:], g_mix[L:L + 1, :].rearrange("o (k p) -> (o k) p", p=128), 8, pp[1], "pp1", "g_rows", "gcol")
                P.stt(DVE, gs[:], modc[:, 8:16], 1.0, gcol[:], ALU.add, ALU.mult, ["modc", "gcol"], ["gs"])
                for i, (c0, w_) in enumerate(((C_AV, 384), (C_RK, 384), (C_RV, 384), (C_HI, 256))):
                    o0 = (0, 384, 768, 1152)[i]
                    P.ld(POOL, wtm[:, :, o0:o0 + w_],
                         w_in[L, :, c0:c0 + w_].rearrange("(kc kp) n -> kp kc n", kp=128), (), ["wtm"])
                P.ld(SP, cmk[:], T["cmask"][:, :], (), ["cmk"])
                P.memset(POOL, zt[:], 0.0, ["zt"])
                for i_, s_ in enumerate(stg_v):
                    P.memset(POOL, s_[:], 1.0, [f"stgv{i_}"])
                vflat = vaug.rearrange("t h d -> t (h d)")

                ev = RR([ACT, DVE])
                P.ts(DVE, rstd[:], ss3[:], 1.0 / D, EPS, ALU.mult, ALU.add, ["ss3"], ["rstd"])
                P.act(rstd[:], rstd[:], AF.Ln, ["rstd"], ["rstd"])
                P.act(rstd[:], rstd[:], AF.Exp, ["rstd"], ["rstd"], scale=-0.5)
                def norm_tile(t):
                    xtt = xt[t % 3]
                    xnn = xn[t % 3]
                    kx = f"xt{t % 3}"
                    kn = f"xn{t % 3}"
                    P.ld(SP, xtt[:], x_cur[t * 128:(t + 1) * 128, :], (), [kx])
                    P.ts(DVE, xnn[:], xtt[:], rstd[:, t:t + 1], None, ALU.mult, None, [kx, "rstd"], [kn])
                    pt_ = ptr[t % 2]
                    kp = f"ptr{t % 2}"
                    for k in range(8):
                        P.tr(pt_[:, k, :], xnn[:, k * 128:(k + 1) * 128], identb[:], [kn, "identb"], [kp], inc=(k == 7))
                    for k in range(8):
                        e_ = ev()
                        if e_ == ACT:
                            P.act(hT[:, k, t * 128:(t + 1) * 128], pt_[:, k, :], AF.Identity, [kp, "gs", "modc"],
                                  [("hT", t)], scale=gs[:, k:k + 1], bias=modc[:, k:k + 1])
                        else:
                            P.ts(DVE, hT[:, k, t * 128:(t + 1) * 128], pt_[:, k, :], gs[:, k:k + 1], modc[:, k:k + 1],
                                 ALU.mult, ALU.add, [kp, "gs", "modc"], [("hT", t)])
                ppi = RR(range(6))
                sbi = RR(range(8))
                svi = RR(range(2))

                def tm_tile(t):
                    for gi, (o0, w_) in enumerate(((0, 384), (384, 384), (768, 384), (1152, 256))):
                        pi = ppi()
                        p_ = pp[pi]
                        for k in range(8):
                            P.mm(p_[:, 0:w_], hT[:, k, t * 128:(t + 1) * 128], wtm[:, k, o0:o0 + w_], k == 0, k == 7,
                                 [("hT", t), "wtm"], [f"pp{pi}"], inc=(k == 7))
                        e_ = ev()
                        if gi == 0:
                            si = svi()
                            sv_ = stg_v[si]
                            P.cp(e_, sv_[:, :, 0:64], p_[:, 0:384].rearrange("p (h d) -> p h d", h=6), [f"pp{pi}"], [f"stgv{si}"])
                            P.ld(POOL, vaug[VPAD + t * 128:VPAD + (t + 1) * 128, :, :], sv_[:], [f"stgv{si}"], ["vaug"])
                        else:
                            si = sbi()
                            sb_ = stg_b[si]
                            dst = (None, rk, rv, hv)[gi]
                            if gi == 1:
                                if e_ == ACT:
                                    P.act(sb_[:, 0:w_], p_[:, 0:w_], AF.Copy, [f"pp{pi}"], [f"stgb{si}"], scale=0.125)
                                else:
                                    P.ts(DVE, sb_[:, 0:w_], p_[:, 0:w_], 0.125, None, ALU.mult, None, [f"pp{pi}"], [f"stgb{si}"])
                            else:
                                P.cp(e_, sb_[:, 0:w_], p_[:, 0:w_], [f"pp{pi}"], [f"stgb{si}"])
                            P.ld(POOL, dst[t * 128:(t + 1) * 128, :], sb_[:, 0:w_], [f"stgb{si}"], [("tm", gi)])
                            if gi == 3:
                                vb_ = vb4[t % 2]
                                P.tt(POOL, vb_[:], sb_[:, 0:256].rearrange("p (h d) -> p h d", h=4).unsqueeze(2).broadcast_to([128, 4, CPT, 64]),
                                     cmk[:].unsqueeze(1).unsqueeze(3).broadcast_to([128, 4, CPT, 64]), ALU.mult,
                                     [f"stgb{si}", "cmk"], [f"vb4{t % 2}"])
                                P.ld(POOL, hvblk[t], vb_[:], [f"vb4{t % 2}"], ["hvblk"])
                norm_tile(0)
                norm_tile(1)
                for t in range(NT):
                    if t + 2 < NT:
                        norm_tile(t + 2)
                    tm_tile(t)
                fm = []
                for j in range(3):
                    fm.append((C_AQ + j * 128, aqT, j, 0.125, BF16))
                for j in range(3):
                    fm.append((C_AK + j * 128, akT, j, 1.0, BF16))
                for j in range(3):
                    fm.append((C_RQ + j * 128, rqT, j, 1.0, BF16))
                for j in range(3):
                    fm.append((C_RK + j * 128, rkT, j, 0.125, BF16))
                for j in range(3):
                    fm.append((C_RG + j * 128, rgT, j, 1.0, F32))
                for c0, dst in ((C_HQ, hqT), (C_HZF, zfT), (C_HZB, zbT), (C_HG, hgT)):
                    for j in range(2):
                        fm.append((c0 + j * 128, dst, j, 1.0, F32))
                sfi = RR(range(6))
                for ci, (c0, dst, j, scl, dt_) in enumerate(fm):
                    wi = ci % 3
                    w = wfm[wi]
                    P.ld(POOL, w[:], w_in[L, :, c0:c0 + 128].rearrange("(kc kp) n -> kp kc n", kp=128), (), [f"wfm{wi}"])
                    for tb in range(8):
                        pi = ppi()
                        p_ = pp[pi]
                        for k in range(8):
                            P.mm(p_[:], w[:, k, :], hT[:, k, tb * 512:(tb + 1) * 512], k == 0, k == 7,
                                 [f"wfm{wi}"] + [("hT", tb * 4 + q) for q in range(4)], [f"pp{pi}"], inc=(k == 7))
                        if dt_ == BF16:
                            si = sbi()
                            st_, ks = stg_b[si], f"stgb{si}"
                        else:
                            si = sfi()
                            st_, ks = stg_f[si], f"stgf{si}"
                        e_ = ev()
                        if scl != 1.0:
                            if e_ == ACT:
                                P.act(st_[:], p_[:], AF.Copy, [f"pp{pi}"], [ks], scale=scl)
                            else:
                                P.ts(DVE, st_[:], p_[:], scl, None, ALU.mult, None, [f"pp{pi}"], [ks])
                        else:
                            P.cp(e_, st_[:], p_[:], [f"pp{pi}"], [ks])
                        P.ld(SP, dst[j * 128:(j + 1) * 128, tb * 512:(tb + 1) * 512], st_[:], [ks], [("fm", ci)])
                if L == 0:
                    for i in range(8):
                        P.ld(POOL, vflat[i * 128:(i + 1) * 128, :], zt[:], ["zt"], ["vaugpad"])
                        P.ld(POOL, vflat[VPAD + S + i * 128:VPAD + S + (i + 1) * 128, :], zt[:], ["zt"], ["vaugpad"])
                P.barrier()
            if done(f"P{L}"):
                break

            with ExitStack() as ph:
                Bt = P.sb(ph, "Bt", [128, 18, 256], BF16)
                Btf = P.sb(ph, "Btf", [128, 6, 256], F32)
                qT = [P.sb(ph, f"qT{i}", [128, S], BF16) for i in range(2)]
                kTp = [P.sb(ph, f"kTp{i}", [128, S + 2 * VPAD], BF16) for i in range(2)]
                NTa = [P.sb(ph, f"NTa{i}", [128, S], F32) for i in range(2)]
                Vr = [P.sb(ph, f"Vr{i}", [128, 33, 2, 128], BF16) for i in range(2)]
                pT = [[P.sb(ph, f"pT{i}{j}", [128, 2, 2, 128], BF16) for j in range(2)] for i in range(2)]
                ao = [P.sb(ph, f"ao{i}", [64, S], BF16) for i in range(2)]
                rec = P.sb(ph, "rec", [64, S], F32)
                ps_s = [[P.ps(ph, f"ps_s{i}{j}", [128, 2, 2, 128], F32) for j in range(2)] for i in range(2)]
                ps_o = [[P.ps(ph, f"ps_o{i}{j}", [128, 4, 128], F32) for j in range(2)] for i in range(2)]
                for g in range(3):
                    P.ld(SP, Btf[:], T["attb"][:, g * 6:(g + 1) * 6, :], (), ["Btf"])
                    P.act(Bt[:, g * 6:(g + 1) * 6, :], Btf[:], AF.Exp, ["Btf"], ["Bt"])
                vi = 0
                mul_eng = RR([DVE, POOL])
                ucount = [0]
                ocount = [0]
                for hp in range(3):
                    h0 = hp * 2
                    pb = hp % 2
                    q_, k_ = qT[pb], kTp[pb]
                    kq, kk = f"qT{pb}", f"kTp{pb}"
                    P.ld(SP, q_[:], aqT[h0 * 64:(h0 + 2) * 64, :], (), [kq])
                    P.memset(POOL, k_[:, 0:VPAD], 0.0, [kk])
                    P.memset(POOL, k_[:, VPAD + S:], 0.0, [kk])
                    P.ld(SP, k_[:, VPAD:VPAD + S], akT[h0 * 64:(h0 + 2) * 64, :], (), [kk])
                    units = []
                    for ri, r in enumerate((1, 4, 16)):
                        Lr = S // r
                        nq = Lr // 128
                        nb = min(4, nq)
                        for c in range(r):
                            vsel = vi % 2
                            vi += 1
                            for m0 in range(0, nq, nb):
                                for half in range(0, nb, 2):
                                    units.append(dict(ri=ri, r=r, c=c, nq=nq, nb=nb, m0=m0, half=half, vsel=vsel,
                                                      vload=(m0 == 0 and half == 0), last=(half + 2 >= nb)))

                    def stage_qk(u, h0=h0, q_=q_, k_=k_, kq=kq, kk=kk):
                        r, c, ri = u["r"], u["c"], u["ri"]
                        if u["vload"]:
                            V_ = Vr[u["vsel"]]
                            src = vaug[sst(VPAD + c - 64 * r, (u["nq"] + 1) * 128, r), h0:h0 + 2, :]
                            P.ld(SP, V_[:, 0:u["nq"] + 1, :, :], src.rearrange("(j i) h d -> i j h d", i=128), ["vaug", "vaugpad"],
                                 [f"Vr{u['vsel']}"])
                        si = ucount[0] % 2
                        ucount[0] += 1
                        u["si"] = si
                        for bi in range(2):
                            m = u["m0"] + u["half"] + bi
                            for jj in range(2):
                                k0 = VPAD + c + r * (128 * (m + jj) - 64)
                                for hh in range(2):
                                    P.mm(ps_s[si][hh][:, bi, jj, :], k_[hh * 64:(hh + 1) * 64, sst(k0, 128, r)],
                                         q_[hh * 64:(hh + 1) * 64, sst(c + r * 128 * m, 128, r)], True, True,
                                         [kq, kk], [f"ps_s{si}{hh}"], inc=(bi == 1 and jj == 1))
                        for hh in range(2):
                            P.act(pT[si][hh][:], ps_s[si][hh][:], AF.Exp, [f"ps_s{si}{hh}"], [f"pT{si}{hh}"])
                        for hh in range(2):
                            P.tt(mul_eng(), pT[si][hh][:], pT[si][hh][:],
                                 Bt[:, (h0 + hh) * 3 + ri, :].rearrange("p (j i) -> p j i", j=2).unsqueeze(1).broadcast_to([128, 2, 2, 128]),
                                 ALU.mult, [f"pT{si}{hh}", "Bt"], [f"pT{si}{hh}"])

                    def stage_pv(u):
                        r, c, ri = u["r"], u["c"], u["ri"]
                        if u["half"] == 0:
                            stage_pv.oi = ocount[0] % 2
                            ocount[0] += 1
                        oi = stage_pv.oi
                        si = u["si"]
                        V_ = Vr[u["vsel"]]
                        for bi in range(2):
                            m = u["m0"] + u["half"] + bi
                            for jj in range(2):
                                for hh in range(2):
                                    P.mm(ps_o[oi][hh][:, u["half"] + bi, :], V_[:, m + jj, hh, :], pT[si][hh][:, bi, jj, :],
                                         jj == 0, jj == 1, [f"Vr{u['vsel']}", f"pT{si}{hh}"], [f"ps_o{oi}{hh}"],
                                         inc=(bi == 1 and jj == 1))
                        if u["last"]:
                            nb, m0 = u["nb"], u["m0"]
                            for hh in range(2):
                                N_ = NTa[hh]
                                kN = f"NTa{hh}"
                                nview = N_[:, sst(c + r * 128 * m0, 128 * nb, r)]
                                pview = ps_o[oi][hh][:, 0:nb, :].rearrange("p a b -> p (a b)")
                                if ri == 0:
                                    P.cp(DVE, nview, pview, [f"ps_o{oi}{hh}"], [kN])
                                else:
                                    P.tt(DVE, nview, nview, pview, ALU.add, [f"ps_o{oi}{hh}", kN], [kN])

                    stage_qk(units[0])
                    for i_u in range(len(units)):
                        if i_u + 1 < len(units):
                            stage_qk(units[i_u + 1])
                        stage_pv(units[i_u])
                    for hh in range(2):
                        N_ = NTa[hh]
                        kN = f"NTa{hh}"
                        P.act(rec[:], N_[64:128, :], AF.Ln, [kN], ["rec"])
                        P.act(rec[:], rec[:], AF.Exp, ["rec"], ["rec"], scale=-1.0)
                        P.tt(DVE, ao[hh][:], N_[0:64, :], rec[:], ALU.mult, [kN, "rec"], [f"ao{hh}"])
                        P.ld(SP, mixT[(h0 + hh) * 64:(h0 + hh + 1) * 64, :], ao[hh][:], [f"ao{hh}"], [("mixT", h0 + hh)])
                P.barrier()
            if done(f"A{L}"):
                break

            with ExitStack() as ph:
                retd = P.sb(ph, "retd", [128, 6, 128], F32)
                retg = P.sb(ph, "retg", [128, 6, 128], F32)
                retk = P.sb(ph, "retk", [128, 6, 2], F32)
                ones64 = P.sb(ph, "ones64", [128, 128], F32)
                gncol = P.sb(ph, "gncol", [128, 3], F32)
                q2 = [P.sb(ph, f"q2{i}", [128, S], BF16) for i in range(2)]
                qg = [P.sb(ph, f"qg{i}", [128, S], BF16) for i in range(2)]
                kT_ = [P.sb(ph, f"kT_{i}", [64, S], BF16) for i in range(2)]
                ktok = [P.sb(ph, f"ktok{i}", [128, NT, 64], BF16) for i in range(2)]
                vtok = [P.sb(ph, f"vtok{i}", [128, NT, 64], BF16) for i in range(2)]
                kd = [P.sb(ph, f"kd{i}", [128, NT, 128], BF16) for i in range(2)]
                Rall = [P.sb(ph, f"Rall{i}", [128, NT, 64], F32) for i in range(2)]
                Rbf = [P.sb(ph, f"Rbf{i}", [128, NT, 64], BF16) for i in range(2)]
                Usb = [P.sb(ph, f"Usb{i}", [128, NT, 64], F32) for i in range(2)]
                OT = [P.sb(ph, f"OT{i}", [128, S], F32) for i in range(2)]
                pTr = [P.sb(ph, f"pTr{i}", [128, 4, 128], BF16) for i in range(2)]
                gfull = P.sb(ph, "gfull", [128, S], F32)
                sq = [P.sb(ph, f"sq{i}", [128, 512], F32) for i in range(2)]
                rs = [P.sb(ph, f"rs{i}", [128, 512], F32) for i in range(2)]
                ostg = [P.sb(ph, f"ostg{i}", [128, 512], BF16) for i in range(2)]
                ps_u = [P.ps(ph, f"ps_u{i}", [128, 8, 64], F32) for i in range(2)]
                ps_a = [P.ps(ph, f"ps_a{i}", [128, 4, 128], F32) for i in range(2)]
                ps_r = [P.ps(ph, f"ps_r{i}", [128, 512], F32) for i in range(2)]
                ps_n = [P.ps(ph, f"ps_n{i}", [128, 512], F32) for i in range(2)]
                P.ld(SP, retd[:], T["retd"][:, :, :], (), ["retd"])
                P.ld(SP, retg[:], T["retg"][:, :, :], (), ["retg"])
                P.ld(SP, retk[:], T["retk"][:, :, :], (), ["retk"])
                P.ld(SP, ones64[:], T["ones64"][:, :], (), ["ones64"])
                col_load(ph, gncol[:], ret_gn[L:L + 1, :].rearrange("o (k p) -> (o k) p", p=128), 3, ps_r[0], "ps_r0", "gn_rows", "gncol")

                def r_prep(h):
                    p_ = h % 2
                    cdec = tabs["ret_cd"][h]
                    q2_, qg_, kT2, kt_, vt_, kd_, Ra, Rb, Us = q2[p_], qg[p_], kT_[p_], ktok[p_], vtok[p_], kd[p_], Rall[p_], Rbf[p_], Usb[p_]
                    K = lambda n: f"{n}{p_}"
                    P.ld(SP, q2_[0:64, :], rqT[h * 64:(h + 1) * 64, :], (), [K("q2")])
                    P.ld(SP, q2_[64:128, :], rqT[h * 64:(h + 1) * 64, :], (), [K("q2")])
                    P.ld(SP, kT2[:], rkT[h * 64:(h + 1) * 64, :], (), [K("kT_")])
                    P.ld(SP, kt_[:], rk[:, h * 64:(h + 1) * 64].rearrange("(n p) d -> p n d", p=128), (), [K("ktok")])
                    P.ld(SP, vt_[:], rv[:, h * 64:(h + 1) * 64].rearrange("(n p) d -> p n d", p=128), (), [K("vtok")])
                    yield
                    P.tt(DVE, qg_[:].rearrange("p (n i) -> p n i", i=128), q2_[:].rearrange("p (n i) -> p n i", i=128),
                         retg[:, h, :].unsqueeze(1).broadcast_to([128, NT, 128]), ALU.mult, [K("q2"), "retg"], [K("qg")])
                    yield
                    P.ts(DVE, kd_[:, :, 0:64], kt_[:], retk[:, h, 0:1], None, ALU.mult, None, [K("ktok"), "retk"], [K("kd")])
                    P.act(kd_[:, :, 64:128], kt_[:], AF.Copy, [K("ktok"), "retk"], [K("kd")], scale=retk[:, h, 1:2])
                    yield
                    for g_ in range(4):
                        pu = ps_u[g_ % 2]
                        for i_ in range(8):
                            n = g_ * 8 + i_
                            P.mm(pu[:, i_, :], kd_[:, n, :], vt_[:, n, :], True, True, [K("kd"), K("vtok")],
                                 [f"ps_u{g_ % 2}"], inc=(i_ == 7))
                        P.cp(ACT, Us[:, g_ * 8:(g_ + 1) * 8, :], pu[:], [f"ps_u{g_ % 2}"],
                             [(K("Usb"), g_ * 8 + i_) for i_ in range(8)])
                        yield
                    P.memset(POOL, Ra[0:64, 0, :], 0.0, [(K("Rf"), 0)])
                    P.memset(POOL, Ra[64:128, NT - 1, :], 0.0, [(K("Rb"), NT - 1)])
                    for n in range(NT - 1):
                        P.stt(DVE, Ra[0:64, n + 1, :], Ra[0:64, n, :], cdec, Us[0:64, n, :],
                              ALU.mult, ALU.add, [(K("Rf"), n), (K("Usb"), n)], [(K("Rf"), n + 1)])
                        nb_ = NT - 1 - n
                        P.stt(DVE, Ra[64:128, nb_ - 1, :], Ra[64:128, nb_, :], cdec,
                              Us[64:128, nb_, :], ALU.mult, ALU.add, [(K("Rb"), nb_), (K("Usb"), nb_)],
                              [(K("Rb"), nb_ - 1)])
                        if n % 4 == 3:
                            yield
                    P.cp(ACT, Rb[:], Ra[:], [(K("Rf"), n) for n in range(NT)] + [(K("Rb"), n) for n in range(NT)], [K("Rbf")])
                    yield

                def r_out(h):
                    p_ = h % 2
                    hb = h % 2
                    OT_ = OT[(h // 2) % 2]
                    kO = f"OT{(h // 2) % 2}"
                    q2_, qg_, kT2, vt_, Rb = q2[p_], qg[p_], kT_[p_], vtok[p_], Rbf[p_]
                    K = lambda n: f"{n}{p_}"

                    def r_qk(n0):
                        ai = (n0 // 4) % 2
                        pa, pt_ = ps_a[ai], pTr[ai]
                        for b in range(4):
                            n = n0 + b
                            P.mm(pa[:, b, :], kT2[:, n * 128:(n + 1) * 128], q2_[0:64, n * 128:(n + 1) * 128], True, True,
                                 [K("kT_"), K("q2")], [f"ps_a{ai}"], inc=(b == 3))
                        P.tt(DVE, pt_[:], pa[:], retd[:, h, :].unsqueeze(1).broadcast_to([128, 4, 128]), ALU.mult,
                             [f"ps_a{ai}", "retd"], [f"pTr{ai}"])

                    def r_pv(n0):
                        ai = (n0 // 4) % 2
                        pt_ = pTr[ai]
                        po = ps_r[ai]
                        for b in range(4):
                            n = n0 + b
                            P.mm(po[0:64, b * 128:(b + 1) * 128], vt_[:, n, :], pt_[:, b, :], True, False,
                                 [K("vtok"), f"pTr{ai}"], [f"ps_r{ai}"], inc=False)
                            P.mm(po[0:64, b * 128:(b + 1) * 128], Rb[:, n, :], qg_[:, n * 128:(n + 1) * 128], False, True,
                                 [K("Rbf"), K("qg")], [f"ps_r{ai}"], inc=(b == 3))
                        P.cp(ACT, OT_[hb * 64:(hb + 1) * 64, n0 * 128:(n0 + 4) * 128], po[0:64, :], [f"ps_r{ai}"], [(kO, hb)])

                    r_qk(0)
                    for n0 in range(0, NT, 4):
                        if n0 + 4 < NT:
                            r_qk(n0 + 4)
                        r_pv(n0)
                        yield

                def r_norm(pr):
                    OT_ = OT[pr % 2]
                    kO = f"OT{pr % 2}"
                    KO = [(kO, 0), (kO, 1)]
                    P.ld(SP, gfull[:], rgT[pr * 128:(pr + 1) * 128, :], (), ["gfull"])
                    P.act(gfull[:], gfull[:], AF.Silu, ["gfull"], ["gfull"])
                    yield
                    for tb in range(8):
                        i2 = tb % 2
                        sl = slice(tb * 512, (tb + 1) * 512)
                        pmn, pvr = ps_n[0], ps_n[1]
                        P.mm(pmn[:], ones64[:], OT_[:, sl], True, True, ["ones64"] + KO, ["ps_n0"])
                        P.tt(DVE, OT_[:, sl], OT_[:, sl], pmn[:], ALU.subtract, ["ps_n0"] + KO, KO)
                        P.act(sq[i2][:], OT_[:, sl], AF.Square, KO, [f"sq{i2}"])
                        P.mm(pvr[:], ones64[:], sq[i2][:], True, True, ["ones64", f"sq{i2}"], ["ps_n1"])
                        P.ts(DVE, rs[i2][:], pvr[:], EPS, None, ALU.add, None, ["ps_n1"], [f"rs{i2}"])
                        P.act(rs[i2][:], rs[i2][:], AF.Ln, [f"rs{i2}"], [f"rs{i2}"])
                        P.act(rs[i2][:], rs[i2][:], AF.Exp, [f"rs{i2}"], [f"rs{i2}"], scale=-0.5)
                        P.tt(DVE, rs[i2][:], rs[i2][:], OT_[:, sl], ALU.mult, [f"rs{i2}"] + KO, [f"rs{i2}"])
                        P.stt(DVE, ostg[i2][:], rs[i2][:], gncol[:, pr:pr + 1], gfull[:, sl], ALU.mult, ALU.mult,
                              [f"rs{i2}", "gncol", "gfull"], [f"ostg{i2}"])
                        P.ld(POOL, mixT[384 + pr * 128:384 + (pr + 1) * 128, sl], ostg[i2][:], [f"ostg{i2}"], [("mixTr", pr)])
                        yield

                def runr(g):
                    for _ in g:
                        pass

                def chainr(*gens):
                    for g in gens:
                        yield from g

                def inter(ga, gb, na, nb2):
                    da = db = False
                    while not (da and db):
                        for _ in range(na):
                            if not da:
                                try:
                                    next(ga)
                                except StopIteration:
                                    da = True
                        for _ in range(nb2):
                            if not db:
                                try:
                                    next(gb)
                                except StopIteration:
                                    db = True

                runr(r_prep(0))
                for h in range(6):
                    others = []
                    if h + 1 < 6:
                        others.append(r_prep(h + 1))
                    if h % 2 == 0 and h >= 2:
                        others.append(r_norm((h - 2) // 2))
                    if others:
                        inter(r_out(h), chainr(*others), 1, 3)
                    else:
                        runr(r_out(h))
                runr(r_norm(2))
                P.barrier()
            if done(f"R{L}"):
                break

            with ExitStack() as ph:
                big = P.sb(ph, "big", [128, 4 * S], F32)
                Ubuf = P.sb(ph, "Ubuf", [128, 64 * NCH], BF16)
                m01 = P.sb(ph, "m01", [128, S // 2], BF16)
                qtT = [P.sb(ph, f"qtT{i}", [128, S], BF16) for i in range(2)]
                ktT = [P.sb(ph, f"ktT{i}", [128, S], BF16) for i in range(2)]
                khtok = [P.sb(ph, f"khtok{i}", [128, 8, 128], BF16) for i in range(2)]
                Sst = [P.sb(ph, f"Sst{i}", [128, NCH, 64], BF16) for i in range(2)]
                Adec = [P.sb(ph, f"Adec{i}", [128, NCH], F32) for i in range(2)]
                vtk = P.sb(ph, "vtk", [128, NT, 128], BF16)
                OTh = P.sb(ph, "OTh", [128, S], F32)
                vblk = [P.sb(ph, f"vblk{i}", [128, 2, CPT, 64], BF16) for i in range(2)]
                pTh = [[P.sb(ph, f"pTh{i}{j}", [128, 4, 128], BF16) for j in range(2)] for i in range(2)]
                hmask = P.sb(ph, "hmask", [128, 2, 128], F32)
                cmask = P.sb(ph, "cmask", [128, CPT], F32)
                lbt4 = P.sb(ph, "lbt4", [128, 4], F32)
                lbc = P.sb(ph, "lbc", [128, 2], F32)
                omlc = P.sb(ph, "omlc", [128, 2], F32)
                lbtmp = P.sb(ph, "lbtmp", [128, 2, 2], F32)
                ps_t = [P.ps(ph, f"ps_t{i}", [128, 8, 128], BF16) for i in range(1)]
                ps_uu = [P.ps(ph, f"ps_uu{i}", [128, CPT, 64], F32) for i in range(3)]
                uui = RR(range(3))
                ps_aa = [P.ps(ph, f"ps_aa{i}", [128, 4, 128], F32) for i in range(2)]
                ps_oo = [P.ps(ph, f"ps_oo{i}", [128, 512], F32) for i in range(2)]
                T0, T1, T2, T3 = (big[:, i * S:(i + 1) * S] for i in range(4))
                khT = P.sb(ph, "khT", [128, S], BF16)
                Uall = Ubuf[:].rearrange("p (d c) -> p d c", c=NCH)
                BIG = ["big0", "big1", "big2", "big3"]
                P.ld(SP, hmask[:], T["hmask"][:, :, :], (), ["hmask"])
                P.ld(SP, cmask[:], T["cmask"][:, :], (), ["cmask"])
                col_load(ph, lbt4[:], lb_log.rearrange("l (k p) -> (l k) p", p=128), 4, ps_oo[0], "ps_oo0", "lb_rows", "lbt")
                if L == 0:
                    P.memset(POOL, lbc[:], 0.0, ["lbc"])
                    P.memset(POOL, omlc[:], 1.0, ["omlc"])
                else:
                    P.tt(DVE, lbtmp[:, :, 0], lbt4[:, 0:2], lbt4[:, 2:4], ALU.subtract, ["lbt"], ["lbtmp"])
                    P.act(lbtmp[:, :, 0], lbtmp[:, :, 0], AF.Exp, ["lbtmp"], ["lbtmp"])
                    P.ts(DVE, lbtmp[:, :, 0], lbtmp[:, :, 0], 1.0, None, ALU.add, None, ["lbtmp"], ["lbtmp"])
                    P.op(DVE, lambda e: e.reciprocal(out=lbc[:], in_=lbtmp[:, :, 0]), ["lbtmp"], ["lbc"])
                    P.ts(DVE, lbc[:], lbc[:], 1.0 - 1e-6, 0.0, ALU.min, ALU.max, ["lbc"], ["lbc"])
                    P.ts(DVE, omlc[:], lbc[:], -1.0, 1.0, ALU.mult, ALU.add, ["lbc"], ["omlc"])
                P.memset(POOL, m01[:], 1.0, ["m01"])
                P.memset(POOL, m01[:, 0:S // 2:CH], 0.0, ["m01"])
                UB = [("big0", 0), ("big0", 1), ("big1", 0), ("big1", 1)]
                UKEYS = [("U", hh_, t_) for hh_ in range(2) for t_ in range(NT)]

                def gen_G(cyc):
                    pr, di = cyc // 2, cyc % 2
                    rev = (di == 1)
                    par = cyc % 2
                    rows = slice(pr * 128, (pr + 1) * 128)
                    qt_, kt_ = qtT[par], ktT[par]
                    kq_, kk_ = f"qtT{par}", f"ktT{par}"

                    def V(ap2d):
                        return ap2d[:, ::-1] if rev else ap2d
                    HS = S // 2
                    hsl = [slice(0, HS), slice(HS, S)]

                    def kh(name, hf):
                        return (name, hf)
                    zsrc = zbT if rev else zfT
                    for hf in range(2):
                        P.ld(SP, T0[:, hsl[hf]], zsrc[rows, hsl[hf]], (), [kh("big0", hf)])
                    for hf in range(2):
                        P.act(T0[:, hsl[hf]], T0[:, hsl[hf]], AF.Sigmoid, [kh("big0", hf)], [kh("big0", hf)])
                    yield
                    for hf in range(2):
                        P.ts(DVE, T0[:, hsl[hf]], T0[:, hsl[hf]], omlc[:, pr:pr + 1], lbc[:, pr:pr + 1], ALU.mult, ALU.add,
                             [kh("big0", hf), "omlc", "lbc"], [kh("big0", hf)])
                    yield
                    for hf in range(2):
                        P.act(T1[:, hsl[hf]], T0[:, hsl[hf]], AF.Ln, [kh("big0", hf)], [kh("big1", hf)])
                    for hf in range(2):
                        P.act(T0[:, hsl[hf]], T0[:, hsl[hf]], AF.Identity, [kh("big0", hf)], [kh("big0", hf)], scale=-1.0, bias=1.0)
                    yield
                    for hf in range(2):
                        P.op(DVE, lambda e, o=V(T2[:, hsl[hf]]), d0=m01[:], d1=V(T1[:, hsl[hf]]): e.tensor_tensor_scan(
                            out=o, data0=d0, data1=d1, initial=0.0, op0=ALU.mult, op1=ALU.add), ["m01", kh("big1", hf)], [kh("big2", hf)])
                    yield
                    bl0 = 0 if rev else CH - 1
                    blast = T2[:, bl0:S:CH]
                    HC = NCH // 2
                    Ad_ = Adec[par]
                    P.act(Ad_[:], blast, AF.Exp, [kh("big2", 0), kh("big2", 1)], [f"Adec{par}"])
                    for hf in range(2):
                        P.ld(SP, T1[:, hsl[hf]], hqT[rows, hsl[hf]], [kh("big2", hf)], [kh("big1", hf)])
                    for hf in range(2):
                        P.act(T3[:, hsl[hf]], T1[:, hsl[hf]], AF.Sigmoid, [kh("big1", hf)], [kh("big3", hf)])
                    yield
                    for hf in range(2):
                        P.tt(DVE, T1[:, hsl[hf]], T1[:, hsl[hf]], T3[:, hsl[hf]], ALU.mult, [kh("big1", hf), kh("big3", hf)], [kh("big1", hf)])
                    yield
                    for hf in range(2):
                        P.act(T3[:, hsl[hf]], T2[:, hsl[hf]], AF.Exp, [kh("big2", hf), kh("big1", hf)], [kh("big3", hf)])
                    yield
                    yield "wait_o"
                    for hf in range(2):
                        P.tt(DVE, qt_[:, hsl[hf]], T3[:, hsl[hf]], T1[:, hsl[hf]], ALU.mult, [kh("big3", hf), kh("big1", hf)], [kq_])
                    yield
                    for hf in range(2):
                        P.act(T3[:, hsl[hf]], T2[:, hsl[hf]], AF.Exp, [kh("big2", hf), kq_], [kh("big3", hf)], scale=-1.0)
                    yield
                    for hf in range(2):
                        P.tt(DVE, kt_[:, hsl[hf]], T3[:, hsl[hf]], T0[:, hsl[hf]], ALU.mult, [kh("big3", hf), kh("big0", hf)], [kk_])
                    yield
                    for hf in range(2):
                        P.tt(DVE, T3[:, hsl[hf]].rearrange("p (c i) -> p c i", i=CH),
                             blast[:, hf * HC:(hf + 1) * HC].unsqueeze(2).broadcast_to([128, HC, CH]),
                             T2[:, hsl[hf]].rearrange("p (c i) -> p c i", i=CH), ALU.subtract, [kh("big2", hf), kk_], [kh("big3", hf)])
                    yield
                    for hf in range(2):
                        P.act(T3[:, hsl[hf]], T3[:, hsl[hf]], AF.Exp, [kh("big3", hf)], [kh("big3", hf)])
                    yield
                    yield "wait_us"
                    for hf in range(2):
                        P.tt(DVE, khT[:, hsl[hf]], T3[:, hsl[hf]], T0[:, hsl[hf]], ALU.mult, [kh("big3", hf), kh("big0", hf)],
                             [("khT", hf)])
                    yield

                def kh1(hf):
                    return ("big1", hf)

                def gen_US(cyc):
                    pr, di = cyc // 2, cyc % 2
                    rev = (di == 1)
                    par = cyc % 2

                    def V(ap2d):
                        return ap2d[:, ::-1] if rev else ap2d
                    evu = RR([ACT, ACT, DVE])
                    for t0 in range(0, NT, 8):
                        g_ = (t0 // 8) % 2
                        for b in range(8):
                            t = t0 + b
                            P.tr(ps_t[0][:, b, :], khT[:, t * 128:(t + 1) * 128], identb[:], [("khT", t // 16), "identb"], ["ps_t"],
                                 inc=(b == 7))
                        P.cp(ACT, khtok[g_][:], ps_t[0][:], ["ps_t"], [f"khtok{g_}"])
                        yield
                        for b in range(8):
                            t = t0 + b
                            vb = vblk[t % 2]
                            kvb = f"vblk{t % 2}"
                            P.ld(SP, vb[:], hvblk[t, :, pr * 2:(pr + 1) * 2, :, :], (), [kvb])
                            for hh in range(2):
                                ui = uui()
                                pu = ps_uu[ui]
                                P.mm(pu[:].rearrange("p c d -> p (c d)"), khtok[g_][:, b, :], vb[:, hh].rearrange("p c d -> p (c d)"),
                                     True, True, [f"khtok{g_}", kvb], [f"ps_uu{ui}"])
                                P.cp(evu(), Uall[hh * 64:(hh + 1) * 64, :, t * CPT:(t + 1) * CPT],
                                     pu[hh * 64:(hh + 1) * 64].rearrange("p c d -> p d c"), [f"ps_uu{ui}"], [("U", hh, t)])
                            yield
                    S_ = Sst[par]
                    for dv in range(64):
                        P.op(DVE, lambda e, o=V(S_[:, :, dv]), d0=V(Adec[par][:]), d1=V(Uall[:, dv, :]): e.tensor_tensor_scan(
                            out=o, data0=d0, data1=d1, initial=0.0, op0=ALU.mult, op1=ALU.add),
                            [f"Adec{par}"] + UKEYS, [f"Sst{par}"])
                        if dv % 8 == 7:
                            yield

                def gen_O(cyc):
                    pr, di = cyc // 2, cyc % 2
                    rev = (di == 1)
                    par = cyc % 2
                    rows = slice(pr * 128, (pr + 1) * 128)
                    qt_, kt_, S_ = qtT[par], ktT[par], Sst[par]
                    kq_, kk_, kS_ = f"qtT{par}", f"ktT{par}", f"Sst{par}"
                    if di == 0:
                        P.ld(ACT, vtk[:], hv[:, pr * 128:(pr + 1) * 128].rearrange("(n p) d -> p n d", p=128), (), ["vtk"])
                    hunits = list(range(0, NT, 4))

                    def h_qk(iu):
                        t0 = hunits[iu]
                        ai = iu % 2
                        for b in range(4):
                            t = t0 + b
                            for hh in range(2):
                                bp = hh * 64
                                P.mm(ps_aa[hh][:, b, :], kt_[bp:bp + 64, t * 128:(t + 1) * 128], qt_[bp:bp + 64, t * 128:(t + 1) * 128],
                                     True, True, [kk_, kq_], [f"ps_aa{hh}"], inc=(b == 3))
                        for hh in range(2):
                            P.tt(DVE, pTh[ai][hh][:], ps_aa[hh][:], hmask[:, di, :].unsqueeze(1).broadcast_to([128, 4, 128]), ALU.mult,
                                 [f"ps_aa{hh}", "hmask"], [f"pTh{ai}{hh}"])

                    def h_pv(iu):
                        t0 = hunits[iu]
                        ai = iu % 2
                        for b in range(4):
                            t = t0 + b
                            lists = []
                            for hh in range(2):
                                bp = hh * 64
                                po = ps_oo[hh]
                                mms = [(vtk[:, t, hh * 64:(hh + 1) * 64], pTh[ai][hh][:, b, :], po[0:64, b * 128:(b + 1) * 128])]
                                for cc in range(CPT):
                                    c = t * CPT + cc
                                    prev = c + 1 if rev else c - 1
                                    if prev < 0 or prev > NCH - 1:
                                        continue
                                    mms.append((S_[bp:bp + 64, prev, :], qt_[bp:bp + 64, c * CH:(c + 1) * CH],
                                                po[0:64, b * 128 + cc * CH:b * 128 + (cc + 1) * CH]))
                                lists.append(mms)
                            nmm = len(lists[0])
                            for i_ in range(nmm):
                                for hh in range(2):
                                    l_, r_, o_ = lists[hh][i_]
                                    P.mm(o_, l_, r_, i_ == 0, i_ == nmm - 1, ["vtk", f"pTh{ai}{hh}", kS_, kq_],
                                         [f"ps_oo{hh}"], inc=(b == 3 and i_ == nmm - 1))
                        for hh in range(2):
                            osl = OTh[hh * 64:(hh + 1) * 64, t0 * 128:(t0 + 4) * 128]
                            if di == 0:
                                P.cp(ACT, osl, ps_oo[hh][0:64, :], [f"ps_oo{hh}"], [("OTh", hh)])
                            else:
                                P.tt(DVE, osl, osl, ps_oo[hh][0:64, :], ALU.add, [f"ps_oo{hh}", ("OTh", hh)], [("OTh", hh)])

                    h_qk(0)
                    for iu in range(len(hunits)):
                        if iu + 1 < len(hunits):
                            h_qk(iu + 1)
                        h_pv(iu)
                        yield
                    if di == 1:
                        P.ld(SP, ohT[rows, :], OTh[:], [("OTh", 0), ("OTh", 1)], [("ohT", pr)])

                def run(g):
                    for _ in g:
                        pass

                def interleave(ga, gb, na, nb_):
                    da = db = False
                    while not (da and db):
                        for _ in range(na):
                            if not da:
                                try:
                                    next(ga)
                                except StopIteration:
                                    da = True
                        for _ in range(nb_):
                            if not db:
                                try:
                                    next(gb)
                                except StopIteration:
                                    db = True

                def chain(*gens):
                    for g in gens:
                        yield from g

                def interleave3(gens, weights):
                    done_ = [g is None for g in gens]
                    blocked = [None, None, None]
                    while not all(done_):
                        for gi, g in enumerate(gens):
                            for _ in range(weights[gi]):
                                if done_[gi]:
                                    break
                                if blocked[gi] == "wait_o" and not done_[0]:
                                    break
                                if blocked[gi] == "wait_us" and not done_[1]:
                                    break
                                blocked[gi] = None
                                try:
                                    tok = next(g)
                                    if tok in ("wait_o", "wait_us"):
                                        blocked[gi] = tok
                                except StopIteration:
                                    done_[gi] = True

                run(gen_G(0))
                for it in range(1, 6):
                    gO = gen_O(it - 2) if 0 <= it - 2 < 4 else None
                    gU = gen_US(it - 1) if 0 <= it - 1 < 4 else None
                    gG = gen_G(it) if it < 4 else None
                    interleave3([gO, gU, gG], [1, 6, 2])
                P.barrier()
            if done(f"H{L}"):
                break
            with ExitStack() as ph:
                ones256 = P.sb(ph, "ones256", [128, 128], F32)
                hgn = P.sb(ph, "hgn", [128, 2], F32)
                ot = [[P.sb(ph, f"ot{i}{p}", [128, 512], F32) for p in range(2)] for i in range(2)]
                sqh = [[P.sb(ph, f"sqh{i}{p}", [128, 512], F32) for p in range(2)] for i in range(2)]
                rsh = [P.sb(ph, f"rsh{i}", [128, 512], F32) for i in range(2)]
                osth = [[P.sb(ph, f"osth{i}{p}", [128, 512], BF16) for p in range(2)] for i in range(2)]
                ps_n = [P.ps(ph, f"ps_n{i}", [128, 512], F32) for i in range(2)]
                P.ld(SP, ones256[:], T["ones256"][:, :], (), ["ones256"])
                col_load(ph, hgn[:], hgrn_gn[L:L + 1, :].rearrange("o (k p) -> (o k) p", p=128), 2, ps_n[0], "ps_n0", "hgn_rows", "hgn")
                hgf = [P.sb(ph, f"hgf{p}", [128, S], F32) for p in range(2)]
                hgs = P.sb(ph, "hgs", [128, S], F32)
                for p in range(2):
                    P.ld(ACT, hgf[p][:], hgT[p * 128:(p + 1) * 128, :], (), [f"hgf{p}"])
                    P.act(hgs[:], hgf[p][:], AF.Sigmoid, [f"hgf{p}"], ["hgs"])
                    P.tt(DVE, hgf[p][:], hgf[p][:], hgs[:], ALU.mult, [f"hgf{p}", "hgs"], [f"hgf{p}"])
                def hn_s1(tb):
                    i2 = tb % 2
                    sl = slice(tb * 512, (tb + 1) * 512)
                    for p in range(2):
                        P.ld(SP, ot[i2][p][:], ohT[p * 128:(p + 1) * 128, sl], (), [f"ot{i2}{p}"])
                        P.act(sqh[i2][p][:], ot[i2][p][:], AF.Square, [f"ot{i2}{p}"], [f"sqh{i2}{p}"])
                        P.mm(ps_n[i2][:], ones256[:], sqh[i2][p][:], p == 0, p == 1, ["ones256", f"sqh{i2}{p}"], [f"ps_n{i2}"],
                             inc=(p == 1))
                    P.ts(DVE, rsh[i2][:], ps_n[i2][:], EPS, None, ALU.add, None, [f"ps_n{i2}"], [f"rsh{i2}"])
                    P.act(rsh[i2][:], rsh[i2][:], AF.Ln, [f"rsh{i2}"], [f"rsh{i2}"])
                    P.act(rsh[i2][:], rsh[i2][:], AF.Exp, [f"rsh{i2}"], [f"rsh{i2}"], scale=-0.5)

                def hn_s2(tb):
                    i2 = tb % 2
                    sl = slice(tb * 512, (tb + 1) * 512)
                    for p in range(2):
                        P.tt(DVE, ot[i2][p][:], ot[i2][p][:], rsh[i2][:], ALU.mult, [f"ot{i2}{p}", f"rsh{i2}"], [f"ot{i2}{p}"])
                        P.stt(DVE, osth[i2][p][:], ot[i2][p][:], hgn[:, p:p + 1], hgf[p][:, sl], ALU.mult, ALU.mult,
                              [f"ot{i2}{p}", "hgn", f"hgf{p}"], [f"osth{i2}{p}"])
                        P.ld(POOL, mixT[768 + p * 128:768 + (p + 1) * 128, sl], osth[i2][p][:], [f"osth{i2}{p}"], [("mixTh", p)])

                hn_s1(0)
                for tb in range(8):
                    if tb + 1 < 8:
                        hn_s1(tb + 1)
                    hn_s2(tb)
                P.barrier()
            if done(f"HN{L}"):
                break

            with ExitStack() as of:
                wgu = P.sb(of, "wgu", [128, 8, 2 * FH], BF16)
                ss2 = P.sb(of, "ss2", [128, NT], F32)
                with ExitStack() as ph:
                    mx = P.sb(ph, "mx", [128, 8, S], BF16)
                    wo = P.sb(ph, "wo", [128, 8, D], BF16)
                    wof = [P.sb(ph, f"wof{i}", [128, D], F32) for i in range(2)]
                    gb = P.sb(ph, "gb", [128, D], F32)
                    junk = P.sb(ph, "junko", [128, D], BF16)
                    xt = [P.sb(ph, f"xo{i}", [128, D], F32) for i in range(3)]
                    pso = [P.ps(ph, f"pso{i}", [128, 2, 512], F32) for i in range(3)]
                    P.ld(SP, gb[:], modscr[L:L + 1, 2 * D:3 * D].broadcast_to([128, D]), (), ["gb"])
                    for k in range(8):
                        P.ld(ACT, mx[:, k, :], mixT[k * 128:(k + 1) * 128, :], (), [("mx", k)])
                        P.ld(SP, wof[k % 2][:], w_out[L, k * 128:(k + 1) * 128, :], (), [f"wof{k % 2}"])
                        P.tt(DVE, wo[:, k, :], wof[k % 2][:], gb[:], ALU.mult, [f"wof{k % 2}", "gb"], ["wo"])
                    for t in range(NT):
                        x_ = xt[t % 3]
                        kx = f"xo{t % 3}"
                        p_ = pso[t % 3]
                        P.ld(SP, x_[:], x_cur[t * 128:(t + 1) * 128, :], (), [kx])
                        for hf in range(2):
                            for k in range(8):
                                P.mm(p_[:, hf, :], mx[:, k, t * 128:(t + 1) * 128], wo[:, k, hf * 512:(hf + 1) * 512], k == 0, k == 7,
                                     [("mx", k), "wo"], [f"pso{t % 3}"], inc=(k == 7 and hf == 1))
                        P.tt(DVE, x_[:], x_[:], p_[:].rearrange("p a b -> p (a b)"), ALU.add, [kx, f"pso{t % 3}"], [kx])
                        P.act(junk[:], x_[:], AF.Square, [kx], ["junko", "ss2"], accum_out=ss2[:, t:t + 1])
                        P.ld(ACT, xa[t * 128:(t + 1) * 128, :], x_[:], [kx], ["xa"])
                        if t == 1:
                            for k in range(8):
                                P.ld(POOL, wgu[:, k, :], w_gu[L, k * 128:(k + 1) * 128, :], (), ["wgu"])
                    P.barrier()
                if done(f"O{L}"):
                    break

                with ExitStack() as ph:
                    modc = P.sb(ph, "modc2", [128, 48], F32)
                    gcol = P.sb(ph, "gcol2", [128, 8], F32)
                    gs = P.sb(ph, "gs2", [128, 8], F32)
                    rstd = P.sb(ph, "rstd2", [128, NT], F32)
                    gb = P.sb(ph, "gb2", [128, D], F32)
                    xf = [P.sb(ph, f"xf{i}", [128, D], F32) for i in range(2)]
                    xr = [P.sb(ph, f"xr{i}", [128, D], F32) for i in range(2)]
                    xn = [P.sb(ph, f"xnf{i}", [128, D], BF16) for i in range(2)]
                    h2T = [P.sb(ph, f"h2T{i}", [128, 8, 512], BF16) for i in range(2)]
                    actT = P.sb(ph, "actT", [128, 22, 512], BF16)
                    wdr = P.sb(ph, "wdr", [128, 22, D], BF16)
                    wds = P.sb(ph, "wds", [128, 2, D], F32)
                    ptr = [P.ps(ph, f"ptrf{i}", [128, 8, 128], BF16) for i in range(1)]
                    pg = [P.ps(ph, f"pg{i}", [128, 512], F32) for i in range(2)]
                    pu = [P.ps(ph, f"pu{i}", [128, 512], F32) for i in range(2)]
                    pd = [P.ps(ph, f"pd{i}", [128, 512], F32) for i in range(3)]
                    pdi = RR(range(3))
                    col_load(ph, modc[:], modscr[L:L + 1, :].rearrange("o (m p) -> (o m) p", p=128), 48, pg[0], "pg0", "mod_rows2", "modc")
                    col_load(ph, gcol[:], g_ffn[L:L + 1, :].rearrange("o (k p) -> (o k) p", p=128), 8, pg[1], "pg1", "g_rows2", "gcol")
                    P.stt(DVE, gs[:], modc[:, 32:40], 1.0, gcol[:], ALU.add, ALU.mult, ["modc", "gcol"], ["gs"])
                    P.ld(SP, gb[:], modscr[L:L + 1, 5 * D:6 * D].broadcast_to([128, D]), (), ["gb"])
                    P.ts(DVE, rstd[:], ss2[:], 1.0 / D, EPS, ALU.mult, ALU.add, ["ss2"], ["rstd"])
                    P.act(rstd[:], rstd[:], AF.Ln, ["rstd"], ["rstd"])
                    P.act(rstd[:], rstd[:], AF.Exp, ["rstd"], ["rstd"], scale=-0.5)
                    for j in range(22):
                        P.ld(SP if j % 2 == 0 else ACT, wds[:, j % 2, :], w_dn[L, j * 128:(j + 1) * 128, :], (), [f"wds{j % 2}"])
                        P.tt(DVE if j % 2 == 0 else POOL, wdr[:, j, :], wds[:, j % 2, :], gb[:], ALU.mult, [f"wds{j % 2}", "gb"], ["wdr"])
                    x_next = xb

                    def norm_tile(tb, q):
                        hT_ = h2T[tb % 2]
                        t = tb * 4 + q
                        x_ = xf[q % 2]
                        kx = f"xf{q % 2}"
                        xnn, kn = xn[q % 2], f"xnf{q % 2}"
                        P.ld(SP, x_[:], xa[t * 128:(t + 1) * 128, :], (), [kx])
                        P.ts(DVE, xnn[:], x_[:], rstd[:, t:t + 1], None, ALU.mult, None, [kx, "rstd"], [kn])
                        for k in range(8):
                            P.tr(ptr[0][:, k, :], xnn[:, k * 128:(k + 1) * 128], identb[:], [kn, "identb"], ["ptrf"], inc=(k == 7))
                        for k in range(8):
                            if k % 2 == 0:
                                P.act(hT_[:, k, q * 128:(q + 1) * 128], ptr[0][:, k, :], AF.Identity, ["ptrf", "gs", "modc"],
                                      [("h2T", tb % 2, q)], scale=gs[:, k:k + 1], bias=modc[:, 24 + k:25 + k])
                            else:
                                P.ts(DVE, hT_[:, k, q * 128:(q + 1) * 128], ptr[0][:, k, :], gs[:, k:k + 1], modc[:, 24 + k:25 + k],
                                     ALU.mult, ALU.add, ["ptrf", "gs", "modc"], [("h2T", tb % 2, q)])

                    def gate_up(tb):
                        hT_ = h2T[tb % 2]
                        hk = [("h2T", tb % 2, q) for q in range(4)]
                        for j in range(22):
                            g_, u_ = pg[j % 2], pu[j % 2]
                            for k in range(8):
                                P.mm(g_[:], wgu[:, k, j * 128:(j + 1) * 128], hT_[:, k, :], k == 0, k == 7, ["wgu"] + hk,
                                     [f"pg{j % 2}"], inc=(k == 7))
                            for k in range(8):
                                P.mm(u_[:], wgu[:, k, FH + j * 128:FH + (j + 1) * 128], hT_[:, k, :], k == 0, k == 7, ["wgu"] + hk,
                                     [f"pu{j % 2}"], inc=(k == 7))
                            s_ = wds[:, j % 2, 0:512]
                            ks_ = f"wds{j % 2}"
                            P.act(s_, g_[:], AF.Silu, [f"pg{j % 2}"], [ks_])
                            P.tt(DVE, actT[:, j, :], s_, u_[:], ALU.mult, [ks_, f"pu{j % 2}"], [("actT", j)])
                            if tb + 1 < 8 and j in (3, 8, 13, 18):
                                norm_tile(tb + 1, (j - 3) // 5)

                    def down(tb):
                        for q in range(4):
                            t = tb * 4 + q
                            x_ = xr[q % 2]
                            kx = f"xr{q % 2}"
                            P.ld(SP, x_[:], xa[t * 128:(t + 1) * 128, :], (), [kx])
                            for hf in range(2):
                                pi = pdi()
                                for j in range(22):
                                    P.mm(pd[pi][:], actT[:, j, q * 128:(q + 1) * 128], wdr[:, j, hf * 512:(hf + 1) * 512],
                                         j == 0, j == 21, [("actT", j), "wdr"], [f"pd{pi}"], inc=(j == 21))
                                P.tt(DVE, x_[:, hf * 512:(hf + 1) * 512], x_[:, hf * 512:(hf + 1) * 512], pd[pi][:], ALU.add,
                                     [kx, f"pd{pi}"], [kx])
                            P.act(xn[q % 2][:], x_[:], AF.Square, [kx], [f"xnf{q % 2}", "ss3"], accum_out=ss3[:, t:t + 1])
                            P.ld(POOL, x_next[t * 128:(t + 1) * 128, :], x_[:], [kx], ["xb"])

                    for q in range(4):
                        norm_tile(0, q)
                    for tb in range(8):
                        gate_up(tb)
                        down(tb)
                    P.barrier()
            x_cur = xb
            if done(f"F{L}"):
                break

        if stop_after is None:
            with ExitStack() as ph:
                rstd = P.sb(ph, "rstd3", [128, NT], F32)
                gfb = P.sb(ph, "gfb", [128, D], F32)
                xt = [P.sb(ph, f"xl{i}", [128, D], F32) for i in range(6)]
                P.ld(SP, gfb[:], g_fin[0:1, :].broadcast_to([128, D]), (), ["gfb"])
                P.ts(DVE, rstd[:], ss3[:], 1.0 / D, EPS, ALU.mult, ALU.add, ["ss3"], ["rstd"])
                P.act(rstd[:], rstd[:], AF.Ln, ["rstd"], ["rstd"])
                P.act(rstd[:], rstd[:], AF.Exp, ["rstd"], ["rstd"], scale=-0.5)
                for t in range(NT):
                    x_ = xt[t % 6]
                    P.ld(SP if t % 2 == 0 else ACT, x_[:], xb[t * 128:(t + 1) * 128, :], (), [f"xl{t % 6}"])
                    P.stt(DVE, x_[:], x_[:], rstd[:, t:t + 1], gfb[:], ALU.mult, ALU.mult,
                          [f"xl{t % 6}", "rstd", "gfb"], [f"xl{t % 6}"])
                    P.ld(POOL, out[t * 128:(t + 1) * 128, :], x_[:], [f"xl{t % 6}"], ["out"])
                P.barrier()
        P.emit()
    return nc


_TABS = None


def make_in_maps(inputs):
    global _TABS
    if _TABS is None:
        _TABS = make_tables()
    shared = {}
    for k in ("w_ada", "b_ada", "g_mix", "w_in", "ret_gn", "hgrn_gn", "hgrn_lb_logits", "w_out", "g_ffn",
              "w_gate_up", "w_down"):
        shared[k] = np.ascontiguousarray(np.asarray(inputs[k], dtype=np.float32))
    shared["g_final"] = np.ascontiguousarray(np.asarray(inputs["g_final"], dtype=np.float32).reshape(1, D))
    for k in TABLE_SHAPES:
        shared["t_" + k] = np.ascontiguousarray(_TABS[k])
    x = np.asarray(inputs["x"], dtype=np.float32)
    c = np.asarray(inputs["c"], dtype=np.float32)
    maps = []
    for b in range(8):
        m = dict(shared)
        m["x"] = np.ascontiguousarray(x[b])
        m["c"] = np.ascontiguousarray(c[b:b + 1])
        maps.append(m)
    return maps


def kernel(**inputs):
    nc = build()
    maps = make_in_maps(inputs)
    res = run_bass_kernel_spmd(nc, maps, core_ids=list(range(8)))
    return np.stack([np.asarray(r["out"], dtype=np.float32) for r in res.results], axis=0)
```

```python
from contextlib import ExitStack
import numpy as np
import concourse.bass as bass
import concourse.mybir as mybir
from concourse.bass_utils import run_bass_kernel_spmd

F32 = mybir.dt.float32
BF16 = mybir.dt.bfloat16
AF = mybir.ActivationFunctionType
ALU = mybir.AluOpType

PE, ACT, DVE, POOL, SP = "tensor", "scalar", "vector", "gpsimd", "sync"
ENGS = (PE, ACT, DVE, POOL, SP)
NDMASEM = 8

D = 1024
S = 4096
NT = 32
DEPTH = 2
INW = 3968
FH = 2816
EPS = 1e-6
C_AQ, C_AK, C_AV, C_RQ, C_RK, C_RV, C_RG, C_HQ, C_HZF, C_HZB, C_HI, C_HG = (
    0, 384, 768, 1152, 1536, 1920, 2304, 2688, 2944, 3200, 3456, 3712)
VPAD = 1024
CH = 32
NCH = S // CH
CPT = 128 // CH


class Prog:
    def __init__(self, nc, es):
        self.nc = nc
        self.es = es
        self.ops = {e: [] for e in ENGS}
        self.cnt = {e: 0 for e in ENGS}
        self.sem = {e: es.enter_context(nc.semaphore("s_" + e)) for e in (PE, ACT, DVE, POOL)}
        self.dsem, self.dcnt, self.dnext = {}, {}, {}
        for q in (SP, POOL, ACT):
            self.dsem[q] = [es.enter_context(nc.semaphore(f"d_{q}_{i}")) for i in range(NDMASEM)]
            self.dcnt[q] = [0] * NDMASEM
            self.dnext[q] = 0
        self.waited = {e: {} for e in ENGS}
        self.last_w = {}
        self.readers = {}
        self.pending_noinc = {e: False for e in ENGS}

    def sb(self, es, name, shape, dt):
        self.uid = getattr(self, "uid", 0) + 1
        return es.enter_context(self.nc.sbuf_tensor(f"{name}_{self.uid}", list(shape), dt))

    def ps(self, es, name, shape, dt):
        self.uid = getattr(self, "uid", 0) + 1
        return es.enter_context(self.nc.psum_tensor(f"{name}_{self.uid}", list(shape), dt))

    def _need(self, eng, sk, val):
        if self.waited[eng].get(sk, 0) >= val:
            return
        self.waited[eng][sk] = val
        self.ops[eng].append(("wait", sk, val))

    def _deps(self, eng, reads, writes):
        toks = []
        for k in reads:
            lw = self.last_w.get(k)
            if lw is not None:
                toks.append(lw)
        for k in writes:
            lw = self.last_w.get(k)
            if lw is not None:
                toks.append(lw)
            toks.extend(self.readers.get(k, ()))
        for t in toks:
            if t[0] == eng and eng == PE:
                continue
            self._need(eng, t[1], t[2])

    def _commit(self, tok, reads, writes):
        for k in writes:
            self.last_w[k] = tok
            self.readers[k] = []
        for k in reads:
            self.readers.setdefault(k, []).append(tok)

    def op(self, eng, fn, reads=(), writes=(), inc=True):
        self._deps(eng, reads, writes)
        val = self.cnt[eng] + 1
        if inc:
            self.cnt[eng] = val
        self.pending_noinc[eng] = not inc
        self.ops[eng].append(("op", fn, inc))
        self._commit((eng, ("c", eng), val), reads, writes)

    def dma(self, q, fn, reads=(), writes=()):
        self._deps(q, reads, writes)
        i = self.dnext[q]
        self.dnext[q] = (i + 1) % NDMASEM
        if self.dcnt[q][i] > 0:
            self._need(q, ("d", q, i), self.dcnt[q][i])
        self.dcnt[q][i] += 16
        self.ops[q].append(("dma", fn, i))
        self._commit(("dma_" + q, ("d", q, i), self.dcnt[q][i]), reads, writes)

    def barrier(self):
        for e in ENGS:
            assert not self.pending_noinc[e], e
        for eng in ENGS:
            for e2 in (PE, ACT, DVE, POOL):
                if e2 != eng and self.cnt[e2] > 0:
                    self._need(eng, ("c", e2), self.cnt[e2])
            for q in self.dsem:
                for i in range(NDMASEM):
                    if self.dcnt[q][i] > 0:
                        self._need(eng, ("d", q, i), self.dcnt[q][i])
        self.last_w = {}
        self.readers = {}
        self.emit()

    def _semh(self, sk):
        return self.sem[sk[1]] if sk[0] == "c" else self.dsem[sk[1]][sk[2]]

    def emit(self):
        with self.nc.Block() as block:
            for eng in ENGS:
                def body(e, ops=self.ops[eng], eng=eng):
                    for o in ops:
                        if o[0] == "wait":
                            e.wait_ge(self._semh(o[1]), o[2])
                        elif o[0] == "op":
                            ins = o[1](e)
                            if o[2]:
                                ins.then_inc(self.sem[eng], 1)
                        else:
                            o[1](e).then_inc(self.dsem[eng][o[2]], 16)
                getattr(block, eng)(body)
        self.ops = {e: [] for e in ENGS}

    def act(self, out, in_, func, reads, writes, eng=ACT, **kw):
        self.op(eng, lambda e: e.activation(out=out, in_=in_, func=func, **kw), reads, writes)

    def tt(self, eng, out, in0, in1, op, reads, writes):
        self.op(eng, lambda e: e.tensor_tensor(out=out, in0=in0, in1=in1, op=op), reads, writes)

    def ts(self, eng, out, in0, s1, s2, op0, op1, reads, writes):
        if s2 is None:
            self.op(eng, lambda e: e.tensor_scalar(out=out, in0=in0, scalar1=s1, scalar2=None, op0=op0), reads, writes)
        else:
            self.op(eng, lambda e: e.tensor_scalar(out=out, in0=in0, scalar1=s1, scalar2=s2, op0=op0, op1=op1), reads, writes)

    def stt(self, eng, out, in0, scalar, in1, op0, op1, reads, writes):
        self.op(eng, lambda e: e.scalar_tensor_tensor(out=out, in0=in0, scalar=scalar, in1=in1, op0=op0, op1=op1), reads, writes)

    def cp(self, eng, out, in_, reads, writes):
        if eng == ACT:
            self.op(eng, lambda e: e.activation(out=out, in_=in_, func=AF.Copy), reads, writes)
        else:
            self.op(eng, lambda e: e.tensor_copy(out=out, in_=in_), reads, writes)

    def mm(self, out, lhsT, rhs, start, stop, reads, writes, inc=True):
        self.op(PE, lambda e: e.matmul(out, lhsT=lhsT, rhs=rhs, start=start, stop=stop), reads, writes, inc=inc)

    def tr(self, out, in_, ident, reads, writes, inc=True):
        self.op(PE, lambda e: e.transpose(out=out, in_=in_, identity=ident), reads, writes, inc=inc)

    def memset(self, eng, ap, val, writes):
        self.op(eng, lambda e: e.memset(ap, val), (), writes)

    def ld(self, q, out, in_, reads, writes, slow=False):
        if slow:
            self.dma(q, lambda e: e.dma_start(out=out, in_=in_, allow_slow_non_contiguous=True), reads, writes)
        else:
            self.dma(q, lambda e: e.dma_start(out=out, in_=in_), reads, writes)


def sst(start, n, step=1):
    return slice(start, start + step * (n - 1) + 1, step)


class RR:
    def __init__(self, items):
        self.items = list(items)
        self.i = 0

    def __call__(self):
        v = self.items[self.i % len(self.items)]
        self.i += 1
        return v


def make_tables():
    t = {}
    t["ident"] = np.eye(128, dtype=np.float32)
    slopes = 2.0 ** (-8.0 * np.arange(1, 7) / 6.0)
    ab = np.zeros((18, 128, 256), np.float32)
    ip = np.arange(128)[:, None]
    iq = np.arange(128)[None, :]
    for h in range(6):
        for ri, r in enumerate((1, 4, 16)):
            for jj in range(2):
                d = (ip - 64 + 128 * jj) - iq
                b = -slopes[h] * r * np.abs(d).astype(np.float64)
                b = np.where(np.abs(d) <= 64, b, -30000.0)
                ab[h * 3 + ri, :, jj * 128:(jj + 1) * 128] = b
    t["attb"] = ab.transpose(1, 0, 2).copy()
    gam = 1.0 - 2.0 ** (-5.0 - np.arange(6))
    lg = np.log1p(-(2.0 ** (-5.0 - np.arange(6))))
    pos = np.arange(128, dtype=np.float64)
    retd = np.zeros((128, 6, 128), np.float32)
    retg = np.zeros((128, 6, 128), np.float32)
    retk = np.zeros((128, 6, 2), np.float32)
    for h in range(6):
        retd[:, h, :] = np.exp(lg[h] * np.abs(pos[:, None] - pos[None, :]))
        retg[0:64, h, :] = np.exp(lg[h] * (pos + 1.0))[None, :]
        retg[64:128, h, :] = np.exp(lg[h] * (128.0 - pos))[None, :]
        retk[:, h, 0] = np.exp(lg[h] * (127.0 - pos))
        retk[:, h, 1] = np.exp(lg[h] * pos)
    t["retd"], t["retg"], t["retk"] = retd, retg, retk
    t["ret_cd"] = [float(np.exp(lg[h] * 128.0)) for h in range(6)]
    s_ = np.arange(128)[:, None]
    t_ = np.arange(128)[None, :]
    same = (s_ // CH) == (t_ // CH)
    hm = np.zeros((128, 2, 128), np.float32)
    hm[:, 0, :] = (same & (s_ <= t_)).astype(np.float32)
    hm[:, 1, :] = (same & (s_ >= t_)).astype(np.float32)
    t["hmask"] = hm
    cm = np.zeros((128, CPT), np.float32)
    cm[np.arange(128), np.arange(128) // CH] = 1.0
    t["cmask"] = cm
    o64 = np.zeros((128, 128), np.float32)
    o64[0:64, 0:64] = 1.0 / 64
    o64[64:128, 64:128] = 1.0 / 64
    t["ones64"] = o64
    t["ones256"] = np.full((128, 128), 1.0 / 256, np.float32)
    return t


TABLE_SHAPES = {"ident": [128, 128], "attb": [128, 18, 256], "retd": [128, 6, 128], "retg": [128, 6, 128],
                "retk": [128, 6, 2], "hmask": [128, 2, 128], "cmask": [128, CPT], "ones64": [128, 128],
                "ones256": [128, 128]}


def build(stop_after=None, dbg=()):
    nc = bass.Bass("TRN2", target_bir_lowering=False)
    tabs = make_tables()

    def din(name, shape):
        return nc.dram_tensor(name, list(shape), F32, kind="ExternalInput").ap()

    def dscr(name, shape, dt):
        kind = "ExternalOutput" if name in dbg else "Internal"
        return nc.dram_tensor(name, list(shape), dt, kind=kind).ap()

    x_in = din("x", [S, D])
    c_in = din("c", [1, D])
    w_ada = din("w_ada", [DEPTH, D, 6 * D])
    b_ada = din("b_ada", [DEPTH, 6 * D])
    g_mix = din("g_mix", [DEPTH, D])
    w_in = din("w_in", [DEPTH, D, INW])
    ret_gn = din("ret_gn", [DEPTH, 384])
    hgrn_gn = din("hgrn_gn", [DEPTH, 256])
    lb_log = din("hgrn_lb_logits", [DEPTH, 256])
    w_out = din("w_out", [DEPTH, D, D])
    g_ffn = din("g_ffn", [DEPTH, D])
    w_gu = din("w_gate_up", [DEPTH, D, 2 * FH])
    w_dn = din("w_down", [DEPTH, FH, D])
    g_fin = din("g_final", [1, D])
    T = {k: din("t_" + k, shp) for k, shp in TABLE_SHAPES.items()}
    out = nc.dram_tensor("out", [S, D], F32, kind="ExternalOutput").ap()

    modscr = dscr("modscr", [DEPTH, 6 * D], F32)
    xa = dscr("xa", [S, D], F32)
    xb = dscr("xb", [S, D], F32)
    aqT = dscr("aqT", [384, S], BF16)
    akT = dscr("akT", [384, S], BF16)
    vaug = dscr("vaug", [S + 2 * VPAD, 6, 128], BF16)
    rqT = dscr("rqT", [384, S], BF16)
    rkT = dscr("rkT", [384, S], BF16)
    rk = dscr("rk", [S, 384], BF16)
    rv = dscr("rv", [S, 384], BF16)
    rgT = dscr("rgT", [384, S], F32)
    hqT = dscr("hqT", [256, S], F32)
    zfT = dscr("zfT", [256, S], F32)
    zbT = dscr("zbT", [256, S], F32)
    hgT = dscr("hgT", [256, S], F32)
    hv = dscr("hv", [S, 256], BF16)
    hvblk = dscr("hvblk", [NT, 128, 4, CPT, 64], BF16)
    ohT = dscr("ohT", [256, S], F32)
    mixT = dscr("mixT", [D, S], BF16)
    wdn_bf = dscr("wdn_bf", [FH, D], BF16)

    with ExitStack() as es:
        P = Prog(nc, es)
        identf = P.sb(es, "identf", [128, 128], F32)
        identb = P.sb(es, "identb", [128, 128], BF16)
        ss3 = P.sb(es, "ss3", [128, NT], F32)
        P.ld(SP, identf[:], T["ident"][:, :], (), ["identf"])
        P.cp(DVE, identb[:], identf[:], ["identf"], ["identb"])
        P.barrier()

        def done(tag):
            return stop_after == tag

        def col_load(ph, dst, src2d, n, ps_ap, ps_key, name, dkey):
            tmp = P.sb(ph, name, [n, 128], F32)
            P.ld(SP, tmp[:], src2d, (), [name])
            P.tr(ps_ap[:, 0:n], tmp[:], identf[0:n, 0:n], [name, "identf"], [ps_key])
            P.cp(DVE, dst, ps_ap[:, 0:n], [ps_key], [dkey])

        finished = False

        with ExitStack() as ph:
            ccol = P.sb(ph, "ccol", [128, 8], F32)
            cond = P.sb(ph, "cond", [128, 8], F32)
            ctmp = P.sb(ph, "ctmp", [128, 8], F32)
            brow = P.sb(ph, "brow", [1, 6 * D], F32)
            mrow = P.sb(ph, "mrow", [1, 6 * D], F32)
            wt = [P.sb(ph, f"wada{i}", [128, 8, 512], F32) for i in range(3)]
            pm = [P.ps(ph, f"pm_m{i}", [128, 512], F32) for i in range(2)]
            col_load(ph, ccol[:], c_in.rearrange("o (k p) -> (o k) p", p=128), 8, pm[0], "pm_m0", "c_rows", "ccol")
            P.act(ctmp[:], ccol[:], AF.Exp, ["ccol"], ["ctmp"], scale=-1.0)
            P.ts(DVE, ctmp[:], ctmp[:], 1.0, None, ALU.add, None, ["ctmp"], ["ctmp"])
            P.op(DVE, lambda e: e.reciprocal(out=ctmp[:], in_=ctmp[:]), ["ctmp"], ["ctmp"])
            P.tt(DVE, cond[:], ccol[:], ctmp[:], ALU.mult, ["ccol", "ctmp"], ["cond"])
            xm = [P.sb(ph, f"xm{i}", [128, D], F32) for i in range(3)]
            jm = P.sb(ph, "jm", [128, D], BF16)
            for t in range(NT):
                P.ld(POOL, xm[t % 3][:], x_in[t * 128:(t + 1) * 128, :], (), [f"xm{t % 3}"])
                P.act(jm[:], xm[t % 3][:], AF.Square, [f"xm{t % 3}"], ["jm", "ss3"], accum_out=ss3[:, t:t + 1])
            it = 0
            for L in range(DEPTH):
                P.ld(SP, brow[:], b_ada[L:L + 1, :], ["mrow_done"], ["brow"])
                for n in range(12):
                    w = wt[it % 3]
                    p_ = pm[it % 2]
                    P.ld(SP, w[:],
                         w_ada[L, :, n * 512:(n + 1) * 512].rearrange("(kc kp) n -> kp kc n", kp=128),
                         (), [f"wada{it % 3}"])
                    for k in range(8):
                        P.mm(p_[0:1, :], cond[:, k:k + 1], w[:, k, :], k == 0, k == 7,
                             ["cond", f"wada{it % 3}"], [f"pm_m{it % 2}"], inc=(k == 7))
                    P.tt(DVE, mrow[0:1, n * 512:(n + 1) * 512], p_[0:1, :], brow[0:1, n * 512:(n + 1) * 512],
                         ALU.add, [f"pm_m{it % 2}", "brow"], ["mrow"])
                    it += 1
                P.ld(SP, modscr[L:L + 1, :], mrow[:], ["mrow"], ["modscr", "mrow_done"])
            P.barrier()
        if done("M"):
            finished = True

        x_cur = x_in
        for L in range(DEPTH):
            if finished:
                break
            with ExitStack() as ph:
                hT = P.sb(ph, "hT", [128, 8, S], BF16)
                modc = P.sb(ph, "modc", [128, 48], F32)
                gcol = P.sb(ph, "gcol", [128, 8], F32)
                gs = P.sb(ph, "gs", [128, 8], F32)
                ss = P.sb(ph, "ss", [128, NT], F32)
                rstd = P.sb(ph, "rstd", [128, NT], F32)
                junk = P.sb(ph, "junk", [128, D], BF16)
                xt = [P.sb(ph, f"xt{i}", [128, D], F32) for i in range(3)]
                xn = [P.sb(ph, f"xn{i}", [128, D], BF16) for i in range(3)]
                wtm = P.sb(ph, "wtm", [128, 8, 1408], BF16)
                wfm = [P.sb(ph, f"wfm{i}", [128, 8, 128], BF16) for i in range(3)]
                stg_b = [P.sb(ph, f"stgb{i}", [128, 512], BF16) for i in range(8)]
                stg_f = [P.sb(ph, f"stgf{i}", [128, 512], F32) for i in range(6)]
                stg_v = [P.sb(ph, f"stgv{i}", [128, 6, 128], BF16) for i in range(2)]
                zt = P.sb(ph, "zt", [128, 768], BF16)
                cmk = P.sb(ph, "cmk", [128, CPT], F32)
                vb4 = [P.sb(ph, f"vb4{i}", [128, 4, CPT, 64], BF16) for i in range(2)]
                ptr = [P.ps(ph, f"ptr{i}", [128, 8, 128], BF16) for i in range(2)]
                pp = [P.ps(ph, f"pp{i}", [128, 512], F32) for i in range(6)]

                col_load(ph, modc[:], modscr[L:L + 1, :].rearrange("o (m p) -> (o m) p", p=128), 48, pp[0], "pp0", "mod_rows", "modc")
                col_load(ph, gcol[:], g_mix[L:L + 1, :].rearrange("o (k p) -> (o k) p", p=128), 8, pp[1], "pp1", "g_rows", "gcol")
                P.stt(DVE, gs[:], modc[:, 8:16], 1.0, gcol[:], ALU.add, ALU.mult, ["modc", "gcol"], ["gs"])
                for i, (c0, w_) in enumerate(((C_AV, 384), (C_RK, 384), (C_RV, 384), (C_HI, 256))):
                    o0 = (0, 384, 768, 1152)[i]
                    P.ld(POOL, wtm[:, :, o0:o0 + w_],
                         w_in[L, :, c0:c0 + w_].rearrange("(kc kp) n -> kp kc n", kp=128), (), ["wtm"])
                P.ld(SP, cmk[:], T["cmask"][:, :], (), ["cmk"])
                P.memset(POOL, zt[:], 0.0, ["zt"])
                for i_, s_ in enumerate(stg_v):
                    P.memset(POOL, s_[:], 1.0, [f"stgv{i_}"])
                vflat = vaug.rearrange("t h d -> t (h d)")

                ev = RR([ACT, DVE])
                P.ts(DVE, rstd[:], ss3[:], 1.0 / D, EPS, ALU.mult, ALU.add, ["ss3"], ["rstd"])
                P.act(rstd[:], rstd[:], AF.Ln, ["rstd"], ["rstd"])
                P.act(rstd[:], rstd[:], AF.Exp, ["rstd"], ["rstd"], scale=-0.5)
                def norm_tile(t):
                    xtt = xt[t % 3]
                    xnn = xn[t % 3]
                    kx = f"xt{t % 3}"
                    kn = f"xn{t % 3}"
                    P.ld(SP, xtt[:], x_cur[t * 128:(t + 1) * 128, :], (), [kx])
                    P.ts(DVE, xnn[:], xtt[:], rstd[:, t:t + 1], None, ALU.mult, None, [kx, "rstd"], [kn])
                    pt_ = ptr[t % 2]
                    kp = f"ptr{t % 2}"
                    for k in range(8):
                        P.tr(pt_[:, k, :], xnn[:, k * 128:(k + 1) * 128], identb[:], [kn, "identb"], [kp], inc=(k == 7))
                    for k in range(8):
                        e_ = ev()
                        if e_ == ACT:
                            P.act(hT[:, k, t * 128:(t + 1) * 128], pt_[:, k, :], AF.Identity, [kp, "gs", "modc"],
                                  [("hT", t)], scale=gs[:, k:k + 1], bias=modc[:, k:k + 1])
                        else:
                            P.ts(DVE, hT[:, k, t * 128:(t + 1) * 128], pt_[:, k, :], gs[:, k:k + 1], modc[:, k:k + 1],
                                 ALU.mult, ALU.add, [kp, "gs", "modc"], [("hT", t)])
                ppi = RR(range(6))
                sbi = RR(range(8))
                svi = RR(range(2))

                def tm_tile(t):
                    for gi, (o0, w_) in enumerate(((0, 384), (384, 384), (768, 384), (1152, 256))):
                        pi = ppi()
                        p_ = pp[pi]
                        for k in range(8):
                            P.mm(p_[:, 0:w_], hT[:, k, t * 128:(t + 1) * 128], wtm[:, k, o0:o0 + w_], k == 0, k == 7,
                                 [("hT", t), "wtm"], [f"pp{pi}"], inc=(k == 7))
                        e_ = ev()
                        if gi == 0:
                            si = svi()
                            sv_ = stg_v[si]
                            P.cp(e_, sv_[:, :, 0:64], p_[:, 0:384].rearrange("p (h d) -> p h d", h=6), [f"pp{pi}"], [f"stgv{si}"])
                            P.ld(POOL, vaug[VPAD + t * 128:VPAD + (t + 1) * 128, :, :], sv_[:], [f"stgv{si}"], ["vaug"])
                        else:
                            si = sbi()
                            sb_ = stg_b[si]
                            dst = (None, rk, rv, hv)[gi]
                            if gi == 1:
                                if e_ == ACT:
                                    P.act(sb_[:, 0:w_], p_[:, 0:w_], AF.Copy, [f"pp{pi}"], [f"stgb{si}"], scale=0.125)
                                else:
                                    P.ts(DVE, sb_[:, 0:w_], p_[:, 0:w_], 0.125, None, ALU.mult, None, [f"pp{pi}"], [f"stgb{si}"])
                            else:
                                P.cp(e_, sb_[:, 0:w_], p_[:, 0:w_], [f"pp{pi}"], [f"stgb{si}"])
                            P.ld(POOL, dst[t * 128:(t + 1) * 128, :], sb_[:, 0:w_], [f"stgb{si}"], [("tm", gi)])
                            if gi == 3:
                                vb_ = vb4[t % 2]
                                P.tt(POOL, vb_[:], sb_[:, 0:256].rearrange("p (h d) -> p h d", h=4).unsqueeze(2).broadcast_to([128, 4, CPT, 64]),
                                     cmk[:].unsqueeze(1).unsqueeze(3).broadcast_to([128, 4, CPT, 64]), ALU.mult,
                                     [f"stgb{si}", "cmk"], [f"vb4{t % 2}"])
                                P.ld(POOL, hvblk[t], vb_[:], [f"vb4{t % 2}"], ["hvblk"])
                norm_tile(0)
                norm_tile(1)
                for t in range(NT):
                    if t + 2 < NT:
                        norm_tile(t + 2)
                    tm_tile(t)
                fm = []
                for j in range(3):
                    fm.append((C_AQ + j * 128, aqT, j, 0.125, BF16))
                for j in range(3):
                    fm.append((C_AK + j * 128, akT, j, 1.0, BF16))
                for j in range(3):
                    fm.append((C_RQ + j * 128, rqT, j, 1.0, BF16))
                for j in range(3):
                    fm.append((C_RK + j * 128, rkT, j, 0.125, BF16))
                for j in range(3):
                    fm.append((C_RG + j * 128, rgT, j, 1.0, F32))
                for c0, dst in ((C_HQ, hqT), (C_HZF, zfT), (C_HZB, zbT), (C_HG, hgT)):
                    for j in range(2):
                        fm.append((c0 + j * 128, dst, j, 1.0, F32))
                sfi = RR(range(6))
                for ci, (c0, dst, j, scl, dt_) in enumerate(fm):
                    wi = ci % 3
                    w = wfm[wi]
                    P.ld(POOL, w[:], w_in[L, :, c0:c0 + 128].rearrange("(kc kp) n -> kp kc n", kp=128), (), [f"wfm{wi}"])
                    for tb in range(8):
                        pi = ppi()
                        p_ = pp[pi]
                        for k in range(8):
                            P.mm(p_[:], w[:, k, :], hT[:, k, tb * 512:(tb + 1) * 512], k == 0, k == 7,
                                 [f"wfm{wi}"] + [("hT", tb * 4 + q) for q in range(4)], [f"pp{pi}"], inc=(k == 7))
                        if dt_ == BF16:
                            si = sbi()
                            st_, ks = stg_b[si], f"stgb{si}"
                        else:
                            si = sfi()
                            st_, ks = stg_f[si], f"stgf{si}"
                        e_ = ev()
                        if scl != 1.0:
                            if e_ == ACT:
                                P.act(st_[:], p_[:], AF.Copy, [f"pp{pi}"], [ks], scale=scl)
                            else:
                                P.ts(DVE, st_[:], p_[:], scl, None, ALU.mult, None, [f"pp{pi}"], [ks])
                        else:
                            P.cp(e_, st_[:], p_[:], [f"pp{pi}"], [ks])
                        P.ld(SP, dst[j * 128:(j + 1) * 128, tb * 512:(tb + 1) * 512], st_[:], [ks], [("fm", ci)])
                if L == 0:
                    for i in range(8):
                        P.ld(POOL, vflat[i * 128:(i + 1) * 128, :], zt[:], ["zt"], ["vaugpad"])
                        P.ld(POOL, vflat[VPAD + S + i * 128:VPAD + S + (i + 1) * 128, :], zt[:], ["zt"], ["vaugpad"])
                P.barrier()
            if done(f"P{L}"):
                break

            with ExitStack() as ph:
                Bt = P.sb(ph, "Bt", [128, 18, 256], BF16)
                Btf = P.sb(ph, "Btf", [128, 6, 256], F32)
                qT = [P.sb(ph, f"qT{i}", [128, S], BF16) for i in range(2)]
                kTp = [P.sb(ph, f"kTp{i}", [128, S + 2 * VPAD], BF16) for i in range(2)]
                NTa = [P.sb(ph, f"NTa{i}", [128, S], F32) for i in range(2)]
                Vr = [P.sb(ph, f"Vr{i}", [128, 33, 2, 128], BF16) for i in range(2)]
                pT = [[P.sb(ph, f"pT{i}{j}", [128, 2, 2, 128], BF16) for j in range(2)] for i in range(2)]
                ao = [P.sb(ph, f"ao{i}", [64, S], BF16) for i in range(2)]
                rec = P.sb(ph, "rec", [64, S], F32)
                ps_s = [[P.ps(ph, f"ps_s{i}{j}", [128, 2, 2, 128], F32) for j in range(2)] for i in range(2)]
                ps_o = [[P.ps(ph, f"ps_o{i}{j}", [128, 4, 128], F32) for j in range(2)] for i in range(2)]
                for g in range(3):
                    P.ld(SP, Btf[:], T["attb"][:, g * 6:(g + 1) * 6, :], (), ["Btf"])
                    P.act(Bt[:, g * 6:(g + 1) * 6, :], Btf[:], AF.Exp, ["Btf"], ["Bt"])
                vi = 0
                mul_eng = RR([DVE, POOL, DVE])
                ucount = [0]
                ocount = [0]
                for hp in range(3):
                    h0 = hp * 2
                    pb = hp % 2
                    q_, k_ = qT[pb], kTp[pb]
                    kq, kk = f"qT{pb}", f"kTp{pb}"
                    P.ld(SP, q_[:], aqT[h0 * 64:(h0 + 2) * 64, :], (), [kq])
                    P.memset(POOL, k_[:, 0:VPAD], 0.0, [kk])
                    P.memset(POOL, k_[:, VPAD + S:], 0.0, [kk])
                    P.ld(SP, k_[:, VPAD:VPAD + S], akT[h0 * 64:(h0 + 2) * 64, :], (), [kk])
                    units = []
                    for ri, r in enumerate((1, 4, 16)):
                        Lr = S // r
                        nq = Lr // 128
                        nb = min(4, nq)
                        for c in range(r):
                            vsel = vi % 2
                            vi += 1
                            for m0 in range(0, nq, nb):
                                for half in range(0, nb, 2):
                                    units.append(dict(ri=ri, r=r, c=c, nq=nq, nb=nb, m0=m0, half=half, vsel=vsel,
                                                      vload=(m0 == 0 and half == 0), last=(half + 2 >= nb)))

                    def stage_qk(u, h0=h0, q_=q_, k_=k_, kq=kq, kk=kk):
                        r, c, ri = u["r"], u["c"], u["ri"]
                        if u["vload"]:
                            V_ = Vr[u["vsel"]]
                            src = vaug[sst(VPAD + c - 64 * r, (u["nq"] + 1) * 128, r), h0:h0 + 2, :]
                            P.ld(SP, V_[:, 0:u["nq"] + 1, :, :], src.rearrange("(j i) h d -> i j h d", i=128), ["vaug", "vaugpad"],
                                 [f"Vr{u['vsel']}"])
                        si = ucount[0] % 2
                        ucount[0] += 1
                        u["si"] = si
                        for bi in range(2):
                            m = u["m0"] + u["half"] + bi
                            for jj in range(2):
                                k0 = VPAD + c + r * (128 * (m + jj) - 64)
                                for hh in range(2):
                                    P.mm(ps_s[si][hh][:, bi, jj, :], k_[hh * 64:(hh + 1) * 64, sst(k0, 128, r)],
                                         q_[hh * 64:(hh + 1) * 64, sst(c + r * 128 * m, 128, r)], True, True,
                                         [kq, kk], [f"ps_s{si}{hh}"], inc=(bi == 1 and jj == 1))
                        for hh in range(2):
                            P.act(pT[si][hh][:], ps_s[si][hh][:], AF.Exp, [f"ps_s{si}{hh}"], [f"pT{si}{hh}"])
                        for hh in range(2):
                            P.tt(mul_eng(), pT[si][hh][:], pT[si][hh][:],
                                 Bt[:, (h0 + hh) * 3 + ri, :].rearrange("p (j i) -> p j i", j=2).unsqueeze(1).broadcast_to([128, 2, 2, 128]),
                                 ALU.mult, [f"pT{si}{hh}", "Bt"], [f"pT{si}{hh}"])

                    def stage_pv(u):
                        r, c, ri = u["r"], u["c"], u["ri"]
                        if u["half"] == 0:
                            stage_pv.oi = ocount[0] % 2
                            ocount[0] += 1
                        oi = stage_pv.oi
                        si = u["si"]
                        V_ = Vr[u["vsel"]]
                        for bi in range(2):
                            m = u["m0"] + u["half"] + bi
                            for jj in range(2):
                                for hh in range(2):
                                    P.mm(ps_o[oi][hh][:, u["half"] + bi, :], V_[:, m + jj, hh, :], pT[si][hh][:, bi, jj, :],
                                         jj == 0, jj == 1, [f"Vr{u['vsel']}", f"pT{si}{hh}"], [f"ps_o{oi}{hh}"],
                                         inc=(bi == 1 and jj == 1))
                        if u["last"]:
                            nb, m0 = u["nb"], u["m0"]
                            for hh in range(2):
                                N_ = NTa[hh]
                                kN = f"NTa{hh}"
                                nview = N_[:, sst(c + r * 128 * m0, 128 * nb, r)]
                                pview = ps_o[oi][hh][:, 0:nb, :].rearrange("p a b -> p (a b)")
                                if ri == 0:
                                    P.cp(DVE, nview, pview, [f"ps_o{oi}{hh}"], [kN])
                                else:
                                    P.tt(DVE, nview, nview, pview, ALU.add, [f"ps_o{oi}{hh}", kN], [kN])

                    stage_qk(units[0])
                    for i_u in range(len(units)):
                        if i_u + 1 < len(units):
                            stage_qk(units[i_u + 1])
                        stage_pv(units[i_u])
                    for hh in range(2):
                        N_ = NTa[hh]
                        kN = f"NTa{hh}"
                        P.act(rec[:], N_[64:128, :], AF.Ln, [kN], ["rec"])
                        P.act(rec[:], rec[:], AF.Exp, ["rec"], ["rec"], scale=-1.0)
                        P.tt(DVE, ao[hh][:], N_[0:64, :], rec[:], ALU.mult, [kN, "rec"], [f"ao{hh}"])
                        P.ld(SP, mixT[(h0 + hh) * 64:(h0 + hh + 1) * 64, :], ao[hh][:], [f"ao{hh}"], [("mixT", h0 + hh)])
                P.barrier()
            if done(f"A{L}"):
                break

            with ExitStack() as ph:
                retd = P.sb(ph, "retd", [128, 6, 128], F32)
                retg = P.sb(ph, "retg", [128, 6, 128], F32)
                retk = P.sb(ph, "retk", [128, 6, 2], F32)
                ones64 = P.sb(ph, "ones64", [128, 128], F32)
                gncol = P.sb(ph, "gncol", [128, 3], F32)
                q2 = [P.sb(ph, f"q2{i}", [128, S], BF16) for i in range(2)]
                qg = [P.sb(ph, f"qg{i}", [128, S], BF16) for i in range(2)]
                kT_ = [P.sb(ph, f"kT_{i}", [64, S], BF16) for i in range(2)]
                ktok = [P.sb(ph, f"ktok{i}", [128, NT, 64], BF16) for i in range(2)]
                vtok = [P.sb(ph, f"vtok{i}", [128, NT, 64], BF16) for i in range(2)]
                kd = [P.sb(ph, f"kd{i}", [128, NT, 128], BF16) for i in range(2)]
                Rall = [P.sb(ph, f"Rall{i}", [128, NT, 64], F32) for i in range(2)]
                Rbf = [P.sb(ph, f"Rbf{i}", [128, NT, 64], BF16) for i in range(2)]
                Usb = [P.sb(ph, f"Usb{i}", [128, NT, 64], F32) for i in range(2)]
                OT = [P.sb(ph, f"OT{i}", [128, S], F32) for i in range(2)]
                pTr = [P.sb(ph, f"pTr{i}", [128, 4, 128], BF16) for i in range(2)]
                gfull = P.sb(ph, "gfull", [128, S], F32)
                sq = [P.sb(ph, f"sq{i}", [128, 512], F32) for i in range(2)]
                rs = [P.sb(ph, f"rs{i}", [128, 512], F32) for i in range(2)]
                ostg = [P.sb(ph, f"ostg{i}", [128, 512], BF16) for i in range(2)]
                ps_u = [P.ps(ph, f"ps_u{i}", [128, 8, 64], F32) for i in range(2)]
                ps_a = [P.ps(ph, f"ps_a{i}", [128, 4, 128], F32) for i in range(2)]
                ps_r = [P.ps(ph, f"ps_r{i}", [128, 512], F32) for i in range(2)]
                ps_n = [P.ps(ph, f"ps_n{i}", [128, 512], F32) for i in range(2)]
                P.ld(SP, retd[:], T["retd"][:, :, :], (), ["retd"])
                P.ld(SP, retg[:], T["retg"][:, :, :], (), ["retg"])
                P.ld(SP, retk[:], T["retk"][:, :, :], (), ["retk"])
                P.ld(SP, ones64[:], T["ones64"][:, :], (), ["ones64"])
                col_load(ph, gncol[:], ret_gn[L:L + 1, :].rearrange("o (k p) -> (o k) p", p=128), 3, ps_r[0], "ps_r0", "gn_rows", "gncol")

                def r_prep(h):
                    p_ = h % 2
                    cdec = tabs["ret_cd"][h]
                    q2_, qg_, kT2, kt_, vt_, kd_, Ra, Rb, Us = q2[p_], qg[p_], kT_[p_], ktok[p_], vtok[p_], kd[p_], Rall[p_], Rbf[p_], Usb[p_]
                    K = lambda n: f"{n}{p_}"
                    P.ld(SP, q2_[0:64, :], rqT[h * 64:(h + 1) * 64, :], (), [K("q2")])
                    P.ld(SP, q2_[64:128, :], rqT[h * 64:(h + 1) * 64, :], (), [K("q2")])
                    P.ld(SP, kT2[:], rkT[h * 64:(h + 1) * 64, :], (), [K("kT_")])
                    P.ld(SP, kt_[:], rk[:, h * 64:(h + 1) * 64].rearrange("(n p) d -> p n d", p=128), (), [K("ktok")])
                    P.ld(SP, vt_[:], rv[:, h * 64:(h + 1) * 64].rearrange("(n p) d -> p n d", p=128), (), [K("vtok")])
                    yield
                    P.tt(DVE, qg_[:].rearrange("p (n i) -> p n i", i=128), q2_[:].rearrange("p (n i) -> p n i", i=128),
                         retg[:, h, :].unsqueeze(1).broadcast_to([128, NT, 128]), ALU.mult, [K("q2"), "retg"], [K("qg")])
                    yield
                    P.ts(DVE, kd_[:, :, 0:64], kt_[:], retk[:, h, 0:1], None, ALU.mult, None, [K("ktok"), "retk"], [K("kd")])
                    P.act(kd_[:, :, 64:128], kt_[:], AF.Copy, [K("ktok"), "retk"], [K("kd")], scale=retk[:, h, 1:2])
                    yield
                    for g_ in range(4):
                        pu = ps_u[g_ % 2]
                        for i_ in range(8):
                            n = g_ * 8 + i_
                            P.mm(pu[:, i_, :], kd_[:, n, :], vt_[:, n, :], True, True, [K("kd"), K("vtok")],
                                 [f"ps_u{g_ % 2}"], inc=(i_ == 7))
                        P.cp(ACT, Us[:, g_ * 8:(g_ + 1) * 8, :], pu[:], [f"ps_u{g_ % 2}"],
                             [(K("Usb"), g_ * 8 + i_) for i_ in range(8)])
                        yield
                    P.memset(POOL, Ra[0:64, 0, :], 0.0, [(K("Rf"), 0)])
                    P.memset(POOL, Ra[64:128, NT - 1, :], 0.0, [(K("Rb"), NT - 1)])
                    for n in range(NT - 1):
                        P.stt(DVE, Ra[0:64, n + 1, :], Ra[0:64, n, :], cdec, Us[0:64, n, :],
                              ALU.mult, ALU.add, [(K("Rf"), n), (K("Usb"), n)], [(K("Rf"), n + 1)])
                        nb_ = NT - 1 - n
                        P.stt(DVE, Ra[64:128, nb_ - 1, :], Ra[64:128, nb_, :], cdec,
                              Us[64:128, nb_, :], ALU.mult, ALU.add, [(K("Rb"), nb_), (K("Usb"), nb_)],
                              [(K("Rb"), nb_ - 1)])
                        if n % 4 == 3:
                            yield
                    P.cp(ACT, Rb[:], Ra[:], [(K("Rf"), n) for n in range(NT)] + [(K("Rb"), n) for n in range(NT)], [K("Rbf")])
                    yield

                def r_out(h):
                    p_ = h % 2
                    hb = h % 2
                    OT_ = OT[(h // 2) % 2]
                    kO = f"OT{(h // 2) % 2}"
                    q2_, qg_, kT2, vt_, Rb = q2[p_], qg[p_], kT_[p_], vtok[p_], Rbf[p_]
                    K = lambda n: f"{n}{p_}"

                    def r_qk(n0):
                        ai = (n0 // 4) % 2
                        pa, pt_ = ps_a[ai], pTr[ai]
                        for b in range(4):
                            n = n0 + b
                            P.mm(pa[:, b, :], kT2[:, n * 128:(n + 1) * 128], q2_[0:64, n * 128:(n + 1) * 128], True, True,
                                 [K("kT_"), K("q2")], [f"ps_a{ai}"], inc=(b == 3))
                        P.tt(DVE, pt_[:], pa[:], retd[:, h, :].unsqueeze(1).broadcast_to([128, 4, 128]), ALU.mult,
                             [f"ps_a{ai}", "retd"], [f"pTr{ai}"])

                    def r_pv(n0):
                        ai = (n0 // 4) % 2
                        pt_ = pTr[ai]
                        po = ps_r[ai]
                        for b in range(4):
                            n = n0 + b
                            P.mm(po[0:64, b * 128:(b + 1) * 128], vt_[:, n, :], pt_[:, b, :], True, False,
                                 [K("vtok"), f"pTr{ai}"], [f"ps_r{ai}"], inc=False)
                            P.mm(po[0:64, b * 128:(b + 1) * 128], Rb[:, n, :], qg_[:, n * 128:(n + 1) * 128], False, True,
                                 [K("Rbf"), K("qg")], [f"ps_r{ai}"], inc=(b == 3))
                        P.cp(ACT, OT_[hb * 64:(hb + 1) * 64, n0 * 128:(n0 + 4) * 128], po[0:64, :], [f"ps_r{ai}"], [(kO, hb)])

                    r_qk(0)
                    for n0 in range(0, NT, 4):
                        if n0 + 4 < NT:
                            r_qk(n0 + 4)
                        r_pv(n0)
                        yield

                def r_norm(pr):
                    OT_ = OT[pr % 2]
                    kO = f"OT{pr % 2}"
                    KO = [(kO, 0), (kO, 1)]
                    P.ld(SP, gfull[:], rgT[pr * 128:(pr + 1) * 128, :], (), ["gfull"])
                    P.act(gfull[:], gfull[:], AF.Silu, ["gfull"], ["gfull"])
                    yield
                    for tb in range(8):
                        i2 = tb % 2
                        sl = slice(tb * 512, (tb + 1) * 512)
                        pmn, pvr = ps_n[0], ps_n[1]
                        P.mm(pmn[:], ones64[:], OT_[:, sl], True, True, ["ones64"] + KO, ["ps_n0"])
                        P.tt(DVE, OT_[:, sl], OT_[:, sl], pmn[:], ALU.subtract, ["ps_n0"] + KO, KO)
                        P.act(sq[i2][:], OT_[:, sl], AF.Square, KO, [f"sq{i2}"])
                        P.mm(pvr[:], ones64[:], sq[i2][:], True, True, ["ones64", f"sq{i2}"], ["ps_n1"])
                        P.ts(DVE, rs[i2][:], pvr[:], EPS, None, ALU.add, None, ["ps_n1"], [f"rs{i2}"])
                        P.act(rs[i2][:], rs[i2][:], AF.Ln, [f"rs{i2}"], [f"rs{i2}"])
                        P.act(rs[i2][:], rs[i2][:], AF.Exp, [f"rs{i2}"], [f"rs{i2}"], scale=-0.5)
                        P.tt(DVE, rs[i2][:], rs[i2][:], OT_[:, sl], ALU.mult, [f"rs{i2}"] + KO, [f"rs{i2}"])
                        P.stt(DVE, ostg[i2][:], rs[i2][:], gncol[:, pr:pr + 1], gfull[:, sl], ALU.mult, ALU.mult,
                              [f"rs{i2}", "gncol", "gfull"], [f"ostg{i2}"])
                        P.ld(POOL, mixT[384 + pr * 128:384 + (pr + 1) * 128, sl], ostg[i2][:], [f"ostg{i2}"], [("mixTr", pr)])
                        yield

                def runr(g):
                    for _ in g:
                        pass

                def chainr(*gens):
                    for g in gens:
                        yield from g

                def inter(ga, gb, na, nb2):
                    da = db = False
                    while not (da and db):
                        for _ in range(na):
                            if not da:
                                try:
                                    next(ga)
                                except StopIteration:
                                    da = True
                        for _ in range(nb2):
                            if not db:
                                try:
                                    next(gb)
                                except StopIteration:
                                    db = True

                runr(r_prep(0))
                for h in range(6):
                    others = []
                    if h + 1 < 6:
                        others.append(r_prep(h + 1))
                    if h % 2 == 0 and h >= 2:
                        others.append(r_norm((h - 2) // 2))
                    if others:
                        inter(r_out(h), chainr(*others), 1, 3)
                    else:
                        runr(r_out(h))
                runr(r_norm(2))
                P.barrier()
            if done(f"R{L}"):
                break

            with ExitStack() as ph:
                big = P.sb(ph, "big", [128, 4 * S], F32)
                Ubuf = P.sb(ph, "Ubuf", [128, 64 * NCH], BF16)
                m01 = P.sb(ph, "m01", [128, S // 2], BF16)
                qtT = [P.sb(ph, f"qtT{i}", [128, S], BF16) for i in range(2)]
                ktT = [P.sb(ph, f"ktT{i}", [128, S], BF16) for i in range(2)]
                khtok = [P.sb(ph, f"khtok{i}", [128, 8, 128], BF16) for i in range(2)]
                Sst = [P.sb(ph, f"Sst{i}", [128, NCH, 64], BF16) for i in range(2)]
                Adec = [P.sb(ph, f"Adec{i}", [128, NCH], F32) for i in range(2)]
                vtk = P.sb(ph, "vtk", [128, NT, 128], BF16)
                OTh = P.sb(ph, "OTh", [128, S], F32)
                vblk = [P.sb(ph, f"vblk{i}", [128, 2, CPT, 64], BF16) for i in range(2)]
                pTh = [[P.sb(ph, f"pTh{i}{j}", [128, 4, 128], BF16) for j in range(2)] for i in range(2)]
                hmask = P.sb(ph, "hmask", [128, 2, 128], F32)
                cmask = P.sb(ph, "cmask", [128, CPT], F32)
                lbt4 = P.sb(ph, "lbt4", [128, 4], F32)
                lbc = P.sb(ph, "lbc", [128, 2], F32)
                omlc = P.sb(ph, "omlc", [128, 2], F32)
                lbtmp = P.sb(ph, "lbtmp", [128, 2, 2], F32)
                ps_t = [P.ps(ph, f"ps_t{i}", [128, 8, 128], BF16) for i in range(1)]
                ps_uu = [P.ps(ph, f"ps_uu{i}", [128, CPT, 64], F32) for i in range(3)]
                uui = RR(range(3))
                ps_aa = [P.ps(ph, f"ps_aa{i}", [128, 4, 128], F32) for i in range(2)]
                ps_oo = [P.ps(ph, f"ps_oo{i}", [128, 512], F32) for i in range(2)]
                T0, T1, T2, T3 = (big[:, i * S:(i + 1) * S] for i in range(4))
                khT = P.sb(ph, "khT", [128, S], BF16)
                Uall = Ubuf[:].rearrange("p (d c) -> p d c", c=NCH)
                BIG = ["big0", "big1", "big2", "big3"]
                P.ld(SP, hmask[:], T["hmask"][:, :, :], (), ["hmask"])
                P.ld(SP, cmask[:], T["cmask"][:, :], (), ["cmask"])
                col_load(ph, lbt4[:], lb_log.rearrange("l (k p) -> (l k) p", p=128), 4, ps_oo[0], "ps_oo0", "lb_rows", "lbt")
                if L == 0:
                    P.memset(POOL, lbc[:], 0.0, ["lbc"])
                    P.memset(POOL, omlc[:], 1.0, ["omlc"])
                else:
                    P.tt(DVE, lbtmp[:, :, 0], lbt4[:, 0:2], lbt4[:, 2:4], ALU.subtract, ["lbt"], ["lbtmp"])
                    P.act(lbtmp[:, :, 0], lbtmp[:, :, 0], AF.Exp, ["lbtmp"], ["lbtmp"])
                    P.ts(DVE, lbtmp[:, :, 0], lbtmp[:, :, 0], 1.0, None, ALU.add, None, ["lbtmp"], ["lbtmp"])
                    P.op(DVE, lambda e: e.reciprocal(out=lbc[:], in_=lbtmp[:, :, 0]), ["lbtmp"], ["lbc"])
                    P.ts(DVE, lbc[:], lbc[:], 1.0 - 1e-6, 0.0, ALU.min, ALU.max, ["lbc"], ["lbc"])
                    P.ts(DVE, omlc[:], lbc[:], -1.0, 1.0, ALU.mult, ALU.add, ["lbc"], ["omlc"])
                P.memset(POOL, m01[:], 1.0, ["m01"])
                P.memset(POOL, m01[:, 0:S // 2:CH], 0.0, ["m01"])
                UB = [("big0", 0), ("big0", 1), ("big1", 0), ("big1", 1)]
                UKEYS = [("U", hh_, t_) for hh_ in range(2) for t_ in range(NT)]

                def gen_G(cyc):
                    pr, di = cyc // 2, cyc % 2
                    rev = (di == 1)
                    par = cyc % 2
                    rows = slice(pr * 128, (pr + 1) * 128)
                    qt_, kt_ = qtT[par], ktT[par]
                    kq_, kk_ = f"qtT{par}", f"ktT{par}"

                    def V(ap2d):
                        return ap2d[:, ::-1] if rev else ap2d
                    HS = S // 2
                    hsl = [slice(0, HS), slice(HS, S)]

                    def kh(name, hf):
                        return (name, hf)
                    zsrc = zbT if rev else zfT
                    for hf in range(2):
                        P.ld(SP, T0[:, hsl[hf]], zsrc[rows, hsl[hf]], (), [kh("big0", hf)])
                    for hf in range(2):
                        P.act(T0[:, hsl[hf]], T0[:, hsl[hf]], AF.Sigmoid, [kh("big0", hf)], [kh("big0", hf)])
                    yield
                    for hf in range(2):
                        P.ts(DVE, T0[:, hsl[hf]], T0[:, hsl[hf]], omlc[:, pr:pr + 1], lbc[:, pr:pr + 1], ALU.mult, ALU.add,
                             [kh("big0", hf), "omlc", "lbc"], [kh("big0", hf)])
                    yield
                    for hf in range(2):
                        P.act(T1[:, hsl[hf]], T0[:, hsl[hf]], AF.Ln, [kh("big0", hf)], [kh("big1", hf)])
                    for hf in range(2):
                        P.act(T0[:, hsl[hf]], T0[:, hsl[hf]], AF.Identity, [kh("big0", hf)], [kh("big0", hf)], scale=-1.0, bias=1.0)
                    yield
                    for hf in range(2):
                        P.op(DVE, lambda e, o=V(T2[:, hsl[hf]]), d0=m01[:], d1=V(T1[:, hsl[hf]]): e.tensor_tensor_scan(
                            out=o, data0=d0, data1=d1, initial=0.0, op0=ALU.mult, op1=ALU.add), ["m01", kh("big1", hf)], [kh("big2", hf)])
                    yield
                    bl0 = 0 if rev else CH - 1
                    blast = T2[:, bl0:S:CH]
                    HC = NCH // 2
                    Ad_ = Adec[par]
                    P.act(Ad_[:], blast, AF.Exp, [kh("big2", 0), kh("big2", 1)], [f"Adec{par}"])
                    for hf in range(2):
                        P.ld(SP, T1[:, hsl[hf]], hqT[rows, hsl[hf]], [kh("big2", hf)], [kh("big1", hf)])
                    for hf in range(2):
                        P.act(T3[:, hsl[hf]], T1[:, hsl[hf]], AF.Sigmoid, [kh("big1", hf)], [kh("big3", hf)])
                    yield
                    for hf in range(2):
                        P.tt(DVE, T1[:, hsl[hf]], T1[:, hsl[hf]], T3[:, hsl[hf]], ALU.mult, [kh("big1", hf), kh("big3", hf)], [kh("big1", hf)])
                    yield
                    for hf in range(2):
                        P.act(T3[:, hsl[hf]], T2[:, hsl[hf]], AF.Exp, [kh("big2", hf), kh("big1", hf)], [kh("big3", hf)])
                    yield
                    yield "wait_o"
                    for hf in range(2):
                        P.tt(DVE, qt_[:, hsl[hf]], T3[:, hsl[hf]], T1[:, hsl[hf]], ALU.mult, [kh("big3", hf), kh("big1", hf)], [kq_])
                    yield
                    for hf in range(2):
                        P.act(T3[:, hsl[hf]], T2[:, hsl[hf]], AF.Exp, [kh("big2", hf), kq_], [kh("big3", hf)], scale=-1.0)
                    yield
                    for hf in range(2):
                        P.tt(DVE, kt_[:, hsl[hf]], T3[:, hsl[hf]], T0[:, hsl[hf]], ALU.mult, [kh("big3", hf), kh("big0", hf)], [kk_])
                    yield
                    for hf in range(2):
                        P.tt(DVE, T3[:, hsl[hf]].rearrange("p (c i) -> p c i", i=CH),
                             blast[:, hf * HC:(hf + 1) * HC].unsqueeze(2).broadcast_to([128, HC, CH]),
                             T2[:, hsl[hf]].rearrange("p (c i) -> p c i", i=CH), ALU.subtract, [kh("big2", hf), kk_], [kh("big3", hf)])
                    yield
                    for hf in range(2):
                        P.act(T3[:, hsl[hf]], T3[:, hsl[hf]], AF.Exp, [kh("big3", hf)], [kh("big3", hf)])
                    yield
                    yield "wait_us"
                    for hf in range(2):
                        P.tt(DVE, khT[:, hsl[hf]], T3[:, hsl[hf]], T0[:, hsl[hf]], ALU.mult, [kh("big3", hf), kh("big0", hf)],
                             [("khT", hf)])
                    yield

                def kh1(hf):
                    return ("big1", hf)

                def gen_US(cyc):
                    pr, di = cyc // 2, cyc % 2
                    rev = (di == 1)
                    par = cyc % 2

                    def V(ap2d):
                        return ap2d[:, ::-1] if rev else ap2d
                    evu = RR([ACT, DVE])
                    for t0 in range(0, NT, 8):
                        g_ = (t0 // 8) % 2
                        for b in range(8):
                            t = t0 + b
                            P.tr(ps_t[0][:, b, :], khT[:, t * 128:(t + 1) * 128], identb[:], [("khT", t // 16), "identb"], ["ps_t"],
                                 inc=(b == 7))
                        P.cp(ACT, khtok[g_][:], ps_t[0][:], ["ps_t"], [f"khtok{g_}"])
                        yield
                        for b in range(8):
                            t = t0 + b
                            vb = vblk[t % 2]
                            kvb = f"vblk{t % 2}"
                            P.ld(SP, vb[:], hvblk[t, :, pr * 2:(pr + 1) * 2, :, :], (), [kvb])
                            for hh in range(2):
                                ui = uui()
                                pu = ps_uu[ui]
                                P.mm(pu[:].rearrange("p c d -> p (c d)"), khtok[g_][:, b, :], vb[:, hh].rearrange("p c d -> p (c d)"),
                                     True, True, [f"khtok{g_}", kvb], [f"ps_uu{ui}"])
                                P.cp(evu(), Uall[hh * 64:(hh + 1) * 64, :, t * CPT:(t + 1) * CPT],
                                     pu[hh * 64:(hh + 1) * 64].rearrange("p c d -> p d c"), [f"ps_uu{ui}"], [("U", hh, t)])
                            yield
                    S_ = Sst[par]
                    for dv in range(64):
                        P.op(DVE, lambda e, o=V(S_[:, :, dv]), d0=V(Adec[par][:]), d1=V(Uall[:, dv, :]): e.tensor_tensor_scan(
                            out=o, data0=d0, data1=d1, initial=0.0, op0=ALU.mult, op1=ALU.add),
                            [f"Adec{par}"] + UKEYS, [f"Sst{par}"])
                        if dv % 8 == 7:
                            yield

                def gen_O(cyc):
                    pr, di = cyc // 2, cyc % 2
                    rev = (di == 1)
                    par = cyc % 2
                    rows = slice(pr * 128, (pr + 1) * 128)
                    qt_, kt_, S_ = qtT[par], ktT[par], Sst[par]
                    kq_, kk_, kS_ = f"qtT{par}", f"ktT{par}", f"Sst{par}"
                    if di == 0:
                        P.ld(ACT, vtk[:], hv[:, pr * 128:(pr + 1) * 128].rearrange("(n p) d -> p n d", p=128), (), ["vtk"])
                    hunits = list(range(0, NT, 4))

                    def h_qk(iu):
                        t0 = hunits[iu]
                        ai = iu % 2
                        for b in range(4):
                            t = t0 + b
                            for hh in range(2):
                                bp = hh * 64
                                P.mm(ps_aa[hh][:, b, :], kt_[bp:bp + 64, t * 128:(t + 1) * 128], qt_[bp:bp + 64, t * 128:(t + 1) * 128],
                                     True, True, [kk_, kq_], [f"ps_aa{hh}"], inc=(b == 3))
                        for hh in range(2):
                            P.tt(DVE, pTh[ai][hh][:], ps_aa[hh][:], hmask[:, di, :].unsqueeze(1).broadcast_to([128, 4, 128]), ALU.mult,
                                 [f"ps_aa{hh}", "hmask"], [f"pTh{ai}{hh}"])

                    def h_pv(iu):
                        t0 = hunits[iu]
                        ai = iu % 2
                        for b in range(4):
                            t = t0 + b
                            lists = []
                            for hh in range(2):
                                bp = hh * 64
                                po = ps_oo[hh]
                                mms = [(vtk[:, t, hh * 64:(hh + 1) * 64], pTh[ai][hh][:, b, :], po[0:64, b * 128:(b + 1) * 128])]
                                for cc in range(CPT):
                                    c = t * CPT + cc
                                    prev = c + 1 if rev else c - 1
                                    if prev < 0 or prev > NCH - 1:
                                        continue
                                    mms.append((S_[bp:bp + 64, prev, :], qt_[bp:bp + 64, c * CH:(c + 1) * CH],
                                                po[0:64, b * 128 + cc * CH:b * 128 + (cc + 1) * CH]))
                                lists.append(mms)
                            nmm = len(lists[0])
                            for i_ in range(nmm):
                                for hh in range(2):
                                    l_, r_, o_ = lists[hh][i_]
                                    P.mm(o_, l_, r_, i_ == 0, i_ == nmm - 1, ["vtk", f"pTh{ai}{hh}", kS_, kq_],
                                         [f"ps_oo{hh}"], inc=(b == 3 and i_ == nmm - 1))
                        for hh in range(2):
                            osl = OTh[hh * 64:(hh + 1) * 64, t0 * 128:(t0 + 4) * 128]
                            if di == 0:
                                P.cp(ACT, osl, ps_oo[hh][0:64, :], [f"ps_oo{hh}"], [("OTh", hh)])
                            else:
                                P.tt(DVE, osl, osl, ps_oo[hh][0:64, :], ALU.add, [f"ps_oo{hh}", ("OTh", hh)], [("OTh", hh)])

                    h_qk(0)
                    for iu in range(len(hunits)):
                        if iu + 1 < len(hunits):
                            h_qk(iu + 1)
                        h_pv(iu)
                        yield
                    if di == 1:
                        P.ld(SP, ohT[rows, :], OTh[:], [("OTh", 0), ("OTh", 1)], [("ohT", pr)])

                def run(g):
                    for _ in g:
                        pass

                def interleave(ga, gb, na, nb_):
                    da = db = False
                    while not (da and db):
                        for _ in range(na):
                            if not da:
                                try:
                                    next(ga)
                                except StopIteration:
                                    da = True
                        for _ in range(nb_):
                            if not db:
                                try:
                                    next(gb)
                                except StopIteration:
                                    db = True

                def chain(*gens):
                    for g in gens:
                        yield from g

                def interleave3(gens, weights):
                    done_ = [g is None for g in gens]
                    blocked = [None, None, None]
                    while not all(done_):
                        for gi, g in enumerate(gens):
                            for _ in range(weights[gi]):
                                if done_[gi]:
                                    break
                                if blocked[gi] == "wait_o" and not done_[0]:
                                    break
                                if blocked[gi] == "wait_us" and not done_[1]:
                                    break
                                blocked[gi] = None
                                try:
                                    tok = next(g)
                                    if tok in ("wait_o", "wait_us"):
                                        blocked[gi] = tok
                                except StopIteration:
                                    done_[gi] = True

                run(gen_G(0))
                for it in range(1, 6):
                    gO = gen_O(it - 2) if 0 <= it - 2 < 4 else None
                    gU = gen_US(it - 1) if 0 <= it - 1 < 4 else None
                    gG = gen_G(it) if it < 4 else None
                    interleave3([gO, gU, gG], [1, 6, 2])
                P.barrier()
            if done(f"H{L}"):
                break
            with ExitStack() as ph:
                ones256 = P.sb(ph, "ones256", [128, 128], F32)
                hgn = P.sb(ph, "hgn", [128, 2], F32)
                ot = [[P.sb(ph, f"ot{i}{p}", [128, 512], F32) for p in range(2)] for i in range(2)]
                sqh = [[P.sb(ph, f"sqh{i}{p}", [128, 512], F32) for p in range(2)] for i in range(2)]
                rsh = [P.sb(ph, f"rsh{i}", [128, 512], F32) for i in range(2)]
                osth = [[P.sb(ph, f"osth{i}{p}", [128, 512], BF16) for p in range(2)] for i in range(2)]
                ps_n = [P.ps(ph, f"ps_n{i}", [128, 512], F32) for i in range(2)]
                P.ld(SP, ones256[:], T["ones256"][:, :], (), ["ones256"])
                col_load(ph, hgn[:], hgrn_gn[L:L + 1, :].rearrange("o (k p) -> (o k) p", p=128), 2, ps_n[0], "ps_n0", "hgn_rows", "hgn")
                hgf = [P.sb(ph, f"hgf{p}", [128, S], F32) for p in range(2)]
                hgs = P.sb(ph, "hgs", [128, S], F32)
                for p in range(2):
                    P.ld(ACT, hgf[p][:], hgT[p * 128:(p + 1) * 128, :], (), [f"hgf{p}"])
                    P.act(hgs[:], hgf[p][:], AF.Sigmoid, [f"hgf{p}"], ["hgs"])
                    P.tt(DVE, hgf[p][:], hgf[p][:], hgs[:], ALU.mult, [f"hgf{p}", "hgs"], [f"hgf{p}"])
                def hn_s1(tb):
                    i2 = tb % 2
                    sl = slice(tb * 512, (tb + 1) * 512)
                    for p in range(2):
                        P.ld(SP, ot[i2][p][:], ohT[p * 128:(p + 1) * 128, sl], (), [f"ot{i2}{p}"])
                        P.act(sqh[i2][p][:], ot[i2][p][:], AF.Square, [f"ot{i2}{p}"], [f"sqh{i2}{p}"])
                        P.mm(ps_n[i2][:], ones256[:], sqh[i2][p][:], p == 0, p == 1, ["ones256", f"sqh{i2}{p}"], [f"ps_n{i2}"],
                             inc=(p == 1))
                    P.ts(DVE, rsh[i2][:], ps_n[i2][:], EPS, None, ALU.add, None, [f"ps_n{i2}"], [f"rsh{i2}"])
                    P.act(rsh[i2][:], rsh[i2][:], AF.Ln, [f"rsh{i2}"], [f"rsh{i2}"])
                    P.act(rsh[i2][:], rsh[i2][:], AF.Exp, [f"rsh{i2}"], [f"rsh{i2}"], scale=-0.5)

                def hn_s2(tb):
                    i2 = tb % 2
                    sl = slice(tb * 512, (tb + 1) * 512)
                    for p in range(2):
                        P.tt(DVE, ot[i2][p][:], ot[i2][p][:], rsh[i2][:], ALU.mult, [f"ot{i2}{p}", f"rsh{i2}"], [f"ot{i2}{p}"])
                        P.stt(DVE, osth[i2][p][:], ot[i2][p][:], hgn[:, p:p + 1], hgf[p][:, sl], ALU.mult, ALU.mult,
                              [f"ot{i2}{p}", "hgn", f"hgf{p}"], [f"osth{i2}{p}"])
                        P.ld(POOL, mixT[768 + p * 128:768 + (p + 1) * 128, sl], osth[i2][p][:], [f"osth{i2}{p}"], [("mixTh", p)])

                hn_s1(0)
                for tb in range(8):
                    if tb + 1 < 8:
                        hn_s1(tb + 1)
                    hn_s2(tb)
                P.barrier()
            if done(f"HN{L}"):
                break

            with ExitStack() as of:
                wgu = P.sb(of, "wgu", [128, 8, 2 * FH], BF16)
                ss2 = P.sb(of, "ss2", [128, NT], F32)
                with ExitStack() as ph:
                    mx = P.sb(ph, "mx", [128, 8, S], BF16)
                    wo = P.sb(ph, "wo", [128, 8, D], BF16)
                    wof = [P.sb(ph, f"wof{i}", [128, D], F32) for i in range(2)]
                    gb = P.sb(ph, "gb", [128, D], F32)
                    junk = P.sb(ph, "junko", [128, D], BF16)
                    xt = [P.sb(ph, f"xo{i}", [128, D], F32) for i in range(3)]
                    pso = [P.ps(ph, f"pso{i}", [128, 2, 512], F32) for i in range(3)]
                    P.ld(SP, gb[:], modscr[L:L + 1, 2 * D:3 * D].broadcast_to([128, D]), (), ["gb"])
                    for k in range(8):
                        P.ld(ACT, mx[:, k, :], mixT[k * 128:(k + 1) * 128, :], (), [("mx", k)])
                        P.ld(SP, wof[k % 2][:], w_out[L, k * 128:(k + 1) * 128, :], (), [f"wof{k % 2}"])
                        P.tt(DVE, wo[:, k, :], wof[k % 2][:], gb[:], ALU.mult, [f"wof{k % 2}", "gb"], ["wo"])
                    for t in range(NT):
                        x_ = xt[t % 3]
                        kx = f"xo{t % 3}"
                        p_ = pso[t % 3]
                        P.ld(SP, x_[:], x_cur[t * 128:(t + 1) * 128, :], (), [kx])
                        for hf in range(2):
                            for k in range(8):
                                P.mm(p_[:, hf, :], mx[:, k, t * 128:(t + 1) * 128], wo[:, k, hf * 512:(hf + 1) * 512], k == 0, k == 7,
                                     [("mx", k), "wo"], [f"pso{t % 3}"], inc=(k == 7 and hf == 1))
                        P.tt(DVE, x_[:], x_[:], p_[:].rearrange("p a b -> p (a b)"), ALU.add, [kx, f"pso{t % 3}"], [kx])
                        P.act(junk[:], x_[:], AF.Square, [kx], ["junko", "ss2"], accum_out=ss2[:, t:t + 1])
                        P.ld(ACT, xa[t * 128:(t + 1) * 128, :], x_[:], [kx], ["xa"])
                        if t == 1:
                            for k in range(8):
                                P.ld(POOL, wgu[:, k, :], w_gu[L, k * 128:(k + 1) * 128, :], (), ["wgu"])
                    P.barrier()
                if done(f"O{L}"):
                    break

                with ExitStack() as ph:
                    modc = P.sb(ph, "modc2", [128, 48], F32)
                    gcol = P.sb(ph, "gcol2", [128, 8], F32)
                    gs = P.sb(ph, "gs2", [128, 8], F32)
                    rstd = P.sb(ph, "rstd2", [128, NT], F32)
                    gb = P.sb(ph, "gb2", [128, D], F32)
                    xf = [P.sb(ph, f"xf{i}", [128, D], F32) for i in range(2)]
                    xr = [P.sb(ph, f"xr{i}", [128, D], F32) for i in range(2)]
                    xn = [P.sb(ph, f"xnf{i}", [128, D], BF16) for i in range(2)]
                    h2T = [P.sb(ph, f"h2T{i}", [128, 8, 512], BF16) for i in range(2)]
                    actT = P.sb(ph, "actT", [128, 22, 512], BF16)
                    wdr = P.sb(ph, "wdr", [128, 22, D], BF16)
                    wds = P.sb(ph, "wds", [128, 2, D], F32)
                    ptr = [P.ps(ph, f"ptrf{i}", [128, 8, 128], BF16) for i in range(1)]
                    pg = [P.ps(ph, f"pg{i}", [128, 512], F32) for i in range(2)]
                    pu = [P.ps(ph, f"pu{i}", [128, 512], F32) for i in range(2)]
                    pd = [P.ps(ph, f"pd{i}", [128, 512], F32) for i in range(3)]
                    pdi = RR(range(3))
                    col_load(ph, modc[:], modscr[L:L + 1, :].rearrange("o (m p) -> (o m) p", p=128), 48, pg[0], "pg0", "mod_rows2", "modc")
                    col_load(ph, gcol[:], g_ffn[L:L + 1, :].rearrange("o (k p) -> (o k) p", p=128), 8, pg[1], "pg1", "g_rows2", "gcol")
                    P.stt(DVE, gs[:], modc[:, 32:40], 1.0, gcol[:], ALU.add, ALU.mult, ["modc", "gcol"], ["gs"])
                    P.ld(SP, gb[:], modscr[L:L + 1, 5 * D:6 * D].broadcast_to([128, D]), (), ["gb"])
                    P.ts(DVE, rstd[:], ss2[:], 1.0 / D, EPS, ALU.mult, ALU.add, ["ss2"], ["rstd"])
                    P.act(rstd[:], rstd[:], AF.Ln, ["rstd"], ["rstd"])
                    P.act(rstd[:], rstd[:], AF.Exp, ["rstd"], ["rstd"], scale=-0.5)
                    for j in range(22):
                        P.ld(SP if j % 2 == 0 else ACT, wds[:, j % 2, :], w_dn[L, j * 128:(j + 1) * 128, :], (), [f"wds{j % 2}"])
                        P.tt(DVE if j % 2 == 0 else POOL, wdr[:, j, :], wds[:, j % 2, :], gb[:], ALU.mult, [f"wds{j % 2}", "gb"], ["wdr"])
                    x_next = xb

                    def norm_tile(tb, q):
                        hT_ = h2T[tb % 2]
                        t = tb * 4 + q
                        x_ = xf[q % 2]
                        kx = f"xf{q % 2}"
                        xnn, kn = xn[q % 2], f"xnf{q % 2}"
                        P.ld(SP, x_[:], xa[t * 128:(t + 1) * 128, :], (), [kx])
                        P.ts(DVE, xnn[:], x_[:], rstd[:, t:t + 1], None, ALU.mult, None, [kx, "rstd"], [kn])
                        for k in range(8):
                            P.tr(ptr[0][:, k, :], xnn[:, k * 128:(k + 1) * 128], identb[:], [kn, "identb"], ["ptrf"], inc=(k == 7))
                        for k in range(8):
                            if k % 2 == 0:
                                P.act(hT_[:, k, q * 128:(q + 1) * 128], ptr[0][:, k, :], AF.Identity, ["ptrf", "gs", "modc"],
                                      [("h2T", tb % 2, q)], scale=gs[:, k:k + 1], bias=modc[:, 24 + k:25 + k])
                            else:
                                P.ts(DVE, hT_[:, k, q * 128:(q + 1) * 128], ptr[0][:, k, :], gs[:, k:k + 1], modc[:, 24 + k:25 + k],
                                     ALU.mult, ALU.add, ["ptrf", "gs", "modc"], [("h2T", tb % 2, q)])

                    def gate_up(tb):
                        hT_ = h2T[tb % 2]
                        hk = [("h2T", tb % 2, q) for q in range(4)]
                        for j in range(22):
                            g_, u_ = pg[j % 2], pu[j % 2]
                            for k in range(8):
                                P.mm(g_[:], wgu[:, k, j * 128:(j + 1) * 128], hT_[:, k, :], k == 0, k == 7, ["wgu"] + hk,
                                     [f"pg{j % 2}"], inc=(k == 7))
                            for k in range(8):
                                P.mm(u_[:], wgu[:, k, FH + j * 128:FH + (j + 1) * 128], hT_[:, k, :], k == 0, k == 7, ["wgu"] + hk,
                                     [f"pu{j % 2}"], inc=(k == 7))
                            s_ = wds[:, j % 2, 0:512]
                            ks_ = f"wds{j % 2}"
                            P.act(s_, g_[:], AF.Silu, [f"pg{j % 2}"], [ks_])
                            P.tt(DVE, actT[:, j, :], s_, u_[:], ALU.mult, [ks_, f"pu{j % 2}"], [("actT", j)])
                            if tb + 1 < 8 and j in (3, 8, 13, 18):
                                norm_tile(tb + 1, (j - 3) // 5)

                    def down(tb):
                        for q in range(4):
                            t = tb * 4 + q
                            x_ = xr[q % 2]
                            kx = f"xr{q % 2}"
                            P.ld(SP, x_[:], xa[t * 128:(t + 1) * 128, :], (), [kx])
                            for hf in range(2):
                                pi = pdi()
                                for j in range(22):
                                    P.mm(pd[pi][:], actT[:, j, q * 128:(q + 1) * 128], wdr[:, j, hf * 512:(hf + 1) * 512],
                                         j == 0, j == 21, [("actT", j), "wdr"], [f"pd{pi}"], inc=(j == 21))
                                P.tt(DVE, x_[:, hf * 512:(hf + 1) * 512], x_[:, hf * 512:(hf + 1) * 512], pd[pi][:], ALU.add,
                                     [kx, f"pd{pi}"], [kx])
                            P.act(xn[q % 2][:], x_[:], AF.Square, [kx], [f"xnf{q % 2}", "ss3"], accum_out=ss3[:, t:t + 1])
                            P.ld(POOL, x_next[t * 128:(t + 1) * 128, :], x_[:], [kx], ["xb"])

                    for q in range(4):
                        norm_tile(0, q)
                    for tb in range(8):
                        gate_up(tb)
                        down(tb)
                    P.barrier()
            x_cur = xb
            if done(f"F{L}"):
                break

        if stop_after is None:
            with ExitStack() as ph:
                rstd = P.sb(ph, "rstd3", [128, NT], F32)
                gfb = P.sb(ph, "gfb", [128, D], F32)
                xt = [P.sb(ph, f"xl{i}", [128, 2, D], F32) for i in range(3)]
                P.ld(SP, gfb[:], g_fin[0:1, :].broadcast_to([128, D]), (), ["gfb"])
                P.ts(DVE, rstd[:], ss3[:], 1.0 / D, EPS, ALU.mult, ALU.add, ["ss3"], ["rstd"])
                P.act(rstd[:], rstd[:], AF.Ln, ["rstd"], ["rstd"])
                P.act(rstd[:], rstd[:], AF.Exp, ["rstd"], ["rstd"], scale=-0.5)
                for i2 in range(NT // 2):
                    x_ = xt[i2 % 3]
                    kx = f"xl{i2 % 3}"
                    P.ld(SP if i2 % 2 == 0 else ACT, x_[:], xb[i2 * 256:(i2 + 1) * 256, :].rearrange("(n p) d -> p n d", p=128), (), [kx])
                    for n in range(2):
                        t = i2 * 2 + n
                        P.stt(DVE, x_[:, n, :], x_[:, n, :], rstd[:, t:t + 1], gfb[:], ALU.mult, ALU.mult,
                              [kx, "rstd", "gfb"], [kx])
                    P.ld(POOL, out[i2 * 256:(i2 + 1) * 256, :].rearrange("(n p) d -> p n d", p=128), x_[:], [kx], ["out"])
                P.barrier()
        P.emit()
    return nc


_TABS = None


def make_in_maps(inputs):
    global _TABS
    if _TABS is None:
        _TABS = make_tables()
    shared = {}
    for k in ("w_ada", "b_ada", "g_mix", "w_in", "ret_gn", "hgrn_gn", "hgrn_lb_logits", "w_out", "g_ffn",
              "w_gate_up", "w_down"):
        shared[k] = np.ascontiguousarray(np.asarray(inputs[k], dtype=np.float32))
    shared["g_final"] = np.ascontiguousarray(np.asarray(inputs["g_final"], dtype=np.float32).reshape(1, D))
    for k in TABLE_SHAPES:
        shared["t_" + k] = np.ascontiguousarray(_TABS[k])
    x = np.asarray(inputs["x"], dtype=np.float32)
    c = np.asarray(inputs["c"], dtype=np.float32)
    maps = []
    for b in range(8):
        m = dict(shared)
        m["x"] = np.ascontiguousarray(x[b])
        m["c"] = np.ascontiguousarray(c[b:b + 1])
        maps.append(m)
    return maps


def kernel(**inputs):
    nc = build()
    maps = make_in_maps(inputs)
    res = run_bass_kernel_spmd(nc, maps, core_ids=list(range(8)))
    return np.stack([np.asarray(r["out"], dtype=np.float32) for r in res.results], axis=0)
```
